# Optimizing a Trainium2 kernel written in Bass

```python
import math
import jax, jax.numpy as jnp
from jax import lax
import numpy as np

D_MODEL = 1024
BATCH = 8
SEQ = 8192
DEPTH = 2

CTX_LEN = 256
GRID_W = 64
QBLOCK = 128
WINDOW = 128
ROPE_THETA = 10000.0
EPS = 1e-6
NEG_INF = -1e30
D_FF = 2816
N_SUBLAYERS = 3
N_BRANCH = 4

HEAD_DIM = 64
GQA_HEADS = 4
GQA_KV_HEADS = 2
MLA_HEADS = 4
MLA_Q_RANK = 256
MLA_KV_RANK = 128
MLA_NOPE = 64
MLA_ROPE = 32
MLA_V = 64
DIFF_HEADS = 4
DIFF_QK = 32
DIFF_V = 2 * DIFF_QK
SWA_HEADS = 4
SWA_KV_HEADS = 2
BRANCH_W = 256

GQA_COLS = (GQA_HEADS + 2 * GQA_KV_HEADS) * HEAD_DIM
MLA_COLS = MLA_Q_RANK + MLA_KV_RANK + MLA_ROPE
DIFF_COLS = DIFF_HEADS * (4 * DIFF_QK + DIFF_V)
SWA_COLS = (SWA_HEADS + 2 * SWA_KV_HEADS) * HEAD_DIM
GATE_COLS = N_BRANCH * D_MODEL
IN_SPLITS = [GQA_COLS, GQA_COLS + MLA_COLS, GQA_COLS + MLA_COLS + DIFF_COLS,
             GQA_COLS + MLA_COLS + DIFF_COLS + SWA_COLS]
IN_COLS = IN_SPLITS[-1] + GATE_COLS

GQA_SCALE = HEAD_DIM ** -0.5
MLA_SCALE = (MLA_NOPE + MLA_ROPE) ** -0.5
DIFF_SCALE = DIFF_QK ** -0.5

kernel_name = 'hybrid_parallel_mixer_dit_block'


def rms_norm(x, g):
    xf = x.astype(jnp.float32)
    y = xf * lax.rsqrt(jnp.mean(xf * xf, axis=-1, keepdims=True) + EPS)
    return (y * g.astype(jnp.float32)).astype(x.dtype)


def adaln(x, g, shift, scale):
    return rms_norm(x, g) * (1 + scale[:, None, :]) + shift[:, None, :]


def swiglu(x, w_gate, w_up, w_down):
    return (jax.nn.silu(x @ w_gate) * (x @ w_up)) @ w_down


def ffn_half(x, mod_s, g_pre, g_post, w_gate, w_up, w_down):
    u = adaln(x, g_pre, mod_s[:, 0], mod_s[:, 1])
    return x + 0.5 * mod_s[:, 2][:, None, :] * rms_norm(swiglu(u, w_gate, w_up, w_down), g_post)


def axial_rope_tables(n_tokens, rot_dim):
    rows = n_tokens // GRID_W
    row = jnp.repeat(jnp.arange(rows, dtype=jnp.float32), GRID_W)
    col = jnp.tile(jnp.arange(GRID_W, dtype=jnp.float32), rows)
    half = rot_dim // 2
    inv_freq = ROPE_THETA ** (-jnp.arange(0, half, 2, dtype=jnp.float32) / half)
    ang_r = row[:, None] * inv_freq[None, :]
    ang_c = col[:, None] * inv_freq[None, :]
    return (jnp.cos(ang_r), jnp.sin(ang_r), jnp.cos(ang_c), jnp.sin(ang_c))


def _rotate(x, cos, sin):
    n = x.shape[-1] // 2
    x1, x2 = x[..., :n], x[..., n:]
    cos, sin = cos[:, None, :], sin[:, None, :]
    return jnp.concatenate([x1 * cos - x2 * sin, x2 * cos + x1 * sin], axis=-1)


def apply_axial_rope(x, rope):
    if rope is None:
        return x
    cos_r, sin_r, cos_c, sin_c = rope
    h = x.shape[-1] // 2
    out = jnp.concatenate([_rotate(x[..., :h], cos_r, sin_r), _rotate(x[..., h:], cos_c, sin_c)], axis=-1)
    return out.astype(x.dtype)


def q_heads(q, n_kv):
    B, T, H, d = q.shape
    return q.reshape(B, T, n_kv, H // n_kv, d).transpose(0, 2, 3, 1, 4)


def kv_heads(k):
    return k.transpose(0, 2, 1, 3)


def merge_heads(o):
    B, K, G, T, d = o.shape
    return o.transpose(0, 3, 1, 2, 4).reshape(B, T, K * G * d)


def gqa_proj(h, n_heads, n_kv, rope, q_gain=None, k_gain=None):
    B, T, _ = h.shape
    q, k, v = jnp.split(h, [n_heads * HEAD_DIM, (n_heads + n_kv) * HEAD_DIM], axis=-1)
    q = q.reshape(B, T, n_heads, HEAD_DIM)
    k = k.reshape(B, T, n_kv, HEAD_DIM)
    v = v.reshape(B, T, n_kv, HEAD_DIM)
    if q_gain is not None:
        q = rms_norm(q, q_gain)
        k = rms_norm(k, k_gain)
    q = apply_axial_rope(q, rope)
    k = apply_axial_rope(k, rope)
    return q_heads(q, n_kv), kv_heads(k), kv_heads(v)


def mla_proj(h, q_gain, kv_gain, w_uq, w_ukv, rope):
    B, T, _ = h.shape
    c_q, c_kv, k_pe = jnp.split(h, [MLA_Q_RANK, MLA_Q_RANK + MLA_KV_RANK], axis=-1)
    q = (rms_norm(c_q, q_gain) @ w_uq).reshape(B, T, MLA_HEADS, MLA_NOPE + MLA_ROPE)
    kv = (rms_norm(c_kv, kv_gain) @ w_ukv).reshape(B, T, MLA_HEADS, MLA_NOPE + MLA_V)
    q_nope, q_pe = q[..., :MLA_NOPE], q[..., MLA_NOPE:]
    k_nope, v = kv[..., :MLA_NOPE], kv[..., MLA_NOPE:]
    q_pe = apply_axial_rope(q_pe, rope)
    k_pe = apply_axial_rope(k_pe[:, :, None, :], rope)
    q = jnp.concatenate([q_nope, q_pe], axis=-1)
    k = jnp.concatenate([k_nope, jnp.broadcast_to(k_pe, (B, T, MLA_HEADS, MLA_ROPE))], axis=-1)
    return q_heads(q, MLA_HEADS), kv_heads(k), kv_heads(v)


def diff_proj(h, rope):
    B, T, _ = h.shape
    q, k, v = jnp.split(h, [2 * DIFF_HEADS * DIFF_QK, 4 * DIFF_HEADS * DIFF_QK], axis=-1)
    q = apply_axial_rope(q.reshape(B, T, 2 * DIFF_HEADS, DIFF_QK), rope).reshape(B, T, DIFF_HEADS, 2, DIFF_QK)
    k = apply_axial_rope(k.reshape(B, T, 2 * DIFF_HEADS, DIFF_QK), rope).reshape(B, T, DIFF_HEADS, 2, DIFF_QK)
    v = v.reshape(B, T, DIFF_HEADS, DIFF_V)
    return (q_heads(q[..., 0, :], DIFF_HEADS), q_heads(q[..., 1, :], DIFF_HEADS),
            kv_heads(k[..., 0, :]), kv_heads(k[..., 1, :]), kv_heads(v))


def sweep_query_blocks(block_fn, *qs):
    S = qs[0].shape[-2]
    nb = S // QBLOCK

    def split(a):
        return jnp.moveaxis(a.reshape(a.shape[:-2] + (nb, QBLOCK, a.shape[-1])), -3, 0)

    out = lax.map(lambda xs: block_fn(xs[0], *xs[1]), (jnp.arange(nb), tuple(split(a) for a in qs)))
    out = jnp.moveaxis(out, 0, -3)
    return out.reshape(out.shape[:-3] + (S, out.shape[-1]))


def _scores(q, k, scale):
    return jnp.einsum('bkgqd,bktd->bkgqt', q, k, preferred_element_type=jnp.float32) * scale


def _weigh(p, v):
    return jnp.einsum('bkgqt,bktd->bkgqd', p.astype(v.dtype), v)


def _sink_column(sink, s):
    return jnp.broadcast_to(sink.astype(jnp.float32)[None, :, :, None, None], s.shape[:-1] + (1,))


def dense_attention(q, k, v, scale, sink=None):
    def block(b, qb):
        s = _scores(qb, k, scale)
        if sink is None:
            return _weigh(jax.nn.softmax(s, axis=-1), v)
        p = jax.nn.softmax(jnp.concatenate([s, _sink_column(sink, s)], axis=-1), axis=-1)[..., :-1]
        return _weigh(p, v)
    return sweep_query_blocks(block, q)


def diff_attention(q1, q2, k1, k2, v, lam, scale):
    def block(b, q1b, q2b):
        p = (jax.nn.softmax(_scores(q1b, k1, scale), axis=-1)
             - lam * jax.nn.softmax(_scores(q2b, k2, scale), axis=-1))
        return _weigh(p, v)
    return sweep_query_blocks(block, q1, q2)


def window_attention(q, k, v, k_ctx, v_ctx, sink, scale):
    S = q.shape[-2]
    pad = ((0, 0), (0, 0), (QBLOCK, QBLOCK), (0, 0))
    k_pad, v_pad = jnp.pad(k, pad), jnp.pad(v, pad)
    qi = jnp.arange(QBLOCK)[:, None]
    kj = jnp.arange(3 * QBLOCK)[None, :]
    band = jnp.abs(kj - qi - QBLOCK) <= WINDOW

    def block(b, qb):
        kb = lax.dynamic_slice_in_dim(k_pad, b * QBLOCK, 3 * QBLOCK, axis=2)
        vb = lax.dynamic_slice_in_dim(v_pad, b * QBLOCK, 3 * QBLOCK, axis=2)
        j = (b - 1) * QBLOCK + kj
        allowed = band & (j >= 0) & (j < S)
        s_lat = jnp.where(allowed, _scores(qb, kb, scale), NEG_INF)
        s_ctx = _scores(qb, k_ctx, scale)
        s = jnp.concatenate([s_ctx, s_lat, _sink_column(sink, s_lat)], axis=-1)
        p = jax.nn.softmax(s, axis=-1)[..., :-1]
        return _weigh(p, jnp.concatenate([v_ctx, vb], axis=2))
    return sweep_query_blocks(block, q)


def diff_finish(o, gain, lam_init):
    return merge_heads(rms_norm(o, gain) * (1 - lam_init))


def diff_lambda_init(layer):
    return 0.8 - 0.6 * math.exp(-0.3 * layer)


def merge_branches(h_gate, outs, w_branch, w_out):
    g = jax.nn.sigmoid(h_gate.reshape(h_gate.shape[:-1] + (N_BRANCH, D_MODEL)))
    y = g[..., 0, :] * (outs[0] @ w_branch[0])
    for i in range(1, N_BRANCH):
        y = y + g[..., i, :] * (outs[i] @ w_branch[i])
    return y @ w_out


def token_mixers(u, u_c, w_in, gqa_q_norm, gqa_k_norm, mla_q_norm, mla_kv_norm, mla_w_uq, mla_w_ukv,
                 diff_lambda, diff_subln, swa_sink, w_branch, w_out, lam_init, ropes, need_ctx):
    h = jnp.split(u @ w_in, IN_SPLITS, axis=-1)
    hc = jnp.split(u_c @ w_in, IN_SPLITS, axis=-1)
    sink = swa_sink.reshape(SWA_KV_HEADS, SWA_HEADS // SWA_KV_HEADS)
    lf = diff_lambda.astype(jnp.float32)
    lam = jnp.exp(jnp.sum(lf[0] * lf[1])) - jnp.exp(jnp.sum(lf[2] * lf[3])) + lam_init

    def cat(a_ctx, a_lat):
        return jnp.concatenate([a_ctx, a_lat], axis=2)

    qa, ka, va = gqa_proj(h[0], GQA_HEADS, GQA_KV_HEADS, ropes[HEAD_DIM], gqa_q_norm, gqa_k_norm)
    qa_c, ka_c, va_c = gqa_proj(hc[0], GQA_HEADS, GQA_KV_HEADS, None, gqa_q_norm, gqa_k_norm)
    qm, km, vm = mla_proj(h[1], mla_q_norm, mla_kv_norm, mla_w_uq, mla_w_ukv, ropes[MLA_ROPE])
    qm_c, km_c, vm_c = mla_proj(hc[1], mla_q_norm, mla_kv_norm, mla_w_uq, mla_w_ukv, None)
    q1, q2, k1, k2, vd = diff_proj(h[2], ropes[DIFF_QK])
    q1_c, q2_c, k1_c, k2_c, vd_c = diff_proj(hc[2], None)
    qs, ks, vs = gqa_proj(h[3], SWA_HEADS, SWA_KV_HEADS, ropes[HEAD_DIM])
    qs_c, ks_c, vs_c = gqa_proj(hc[3], SWA_HEADS, SWA_KV_HEADS, None)

    outs = [merge_heads(dense_attention(qa, cat(ka_c, ka), cat(va_c, va), GQA_SCALE)),
            merge_heads(dense_attention(qm, cat(km_c, km), cat(vm_c, vm), MLA_SCALE)),
            diff_finish(diff_attention(q1, q2, cat(k1_c, k1), cat(k2_c, k2), cat(vd_c, vd), lam, DIFF_SCALE),
                        diff_subln, lam_init),
            merge_heads(window_attention(qs, ks, vs, ks_c, vs_c, sink, GQA_SCALE))]
    y = merge_branches(h[4], outs, w_branch, w_out)
    if not need_ctx:
        return y, None
    outs_c = [merge_heads(dense_attention(qa_c, ka_c, va_c, GQA_SCALE)),
              merge_heads(dense_attention(qm_c, km_c, vm_c, MLA_SCALE)),
              diff_finish(diff_attention(q1_c, q2_c, k1_c, k2_c, vd_c, lam, DIFF_SCALE), diff_subln, lam_init),
              merge_heads(dense_attention(qs_c, ks_c, vs_c, GQA_SCALE, sink))]
    return y, merge_branches(hc[4], outs_c, w_branch, w_out)


def setup_inputs(seed: int = 0) -> dict:
    key = jax.random.key(seed)
    ks = jax.random.split(key, 23)

    def nrm(k, shape, s):
        return s * jax.random.normal(k, shape, jnp.float32)

    def gain(k, shape):
        return 1.0 + 0.05 * jax.random.normal(k, shape, jnp.float32)

    D = D_MODEL
    return {
        'x': nrm(ks[0], (BATCH, SEQ, D), 1.0),
        'c': nrm(ks[1], (BATCH, D), 1.0),
        'ctx': nrm(ks[2], (BATCH, CTX_LEN, D), 1.0),
        'c_ctx': nrm(ks[3], (D,), 1.0),
        'w_mod': nrm(ks[4], (DEPTH, D, N_SUBLAYERS * 3 * D), 0.5 * D ** -0.5),
        'b_mod': nrm(ks[5], (DEPTH, N_SUBLAYERS * 3 * D), 0.02),
        'g_pre': gain(ks[6], (DEPTH, N_SUBLAYERS, D)),
        'g_post': gain(ks[7], (DEPTH, N_SUBLAYERS, D)),
        'w_ffn_gate': nrm(ks[8], (DEPTH, 2, D, D_FF), D ** -0.5),
        'w_ffn_up': nrm(ks[9], (DEPTH, 2, D, D_FF), D ** -0.5),
        'w_ffn_down': nrm(ks[10], (DEPTH, 2, D_FF, D), D_FF ** -0.5),
        'w_in': nrm(ks[11], (DEPTH, D, IN_COLS), D ** -0.5),
        'gqa_q_norm': gain(ks[12], (DEPTH, HEAD_DIM)),
        'gqa_k_norm': gain(ks[13], (DEPTH, HEAD_DIM)),
        'mla_q_norm': gain(ks[14], (DEPTH, MLA_Q_RANK)),
        'mla_kv_norm': gain(ks[15], (DEPTH, MLA_KV_RANK)),
        'mla_w_uq': nrm(ks[16], (DEPTH, MLA_Q_RANK, MLA_HEADS * (MLA_NOPE + MLA_ROPE)), MLA_Q_RANK ** -0.5),
        'mla_w_ukv': nrm(ks[17], (DEPTH, MLA_KV_RANK, MLA_HEADS * (MLA_NOPE + MLA_V)), MLA_KV_RANK ** -0.5),
        'diff_lambda': nrm(ks[18], (DEPTH, 4, DIFF_QK), 0.1),
        'diff_subln': gain(ks[19], (DEPTH, DIFF_V)),
        'swa_sink': nrm(ks[20], (DEPTH, SWA_HEADS), 0.5),
        'w_branch': nrm(ks[21], (DEPTH, N_BRANCH, BRANCH_W, D), BRANCH_W ** -0.5),
        'w_out': nrm(ks[22], (DEPTH, D, D), D ** -0.5),
    }


def reference(x, c, ctx, c_ctx, w_mod, b_mod, g_pre, g_post, w_ffn_gate, w_ffn_up, w_ffn_down, w_in,
              gqa_q_norm, gqa_k_norm, mla_q_norm, mla_kv_norm, mla_w_uq, mla_w_ukv,
              diff_lambda, diff_subln, swa_sink, w_branch, w_out):
    S = x.shape[1]
    ropes = {d: axial_rope_tables(S, d) for d in (HEAD_DIM, MLA_ROPE, DIFF_QK)}
    silu_c = jax.nn.silu(c)
    silu_cc = jax.nn.silu(c_ctx)[None, :]
    for l in range(DEPTH):
        need_ctx = l < DEPTH - 1
        mod = (silu_c @ w_mod[l] + b_mod[l]).reshape(-1, N_SUBLAYERS, 3, D_MODEL)
        mod_c = (silu_cc @ w_mod[l] + b_mod[l]).reshape(1, N_SUBLAYERS, 3, D_MODEL)
        ffn1 = (w_ffn_gate[l, 0], w_ffn_up[l, 0], w_ffn_down[l, 0])
        ffn2 = (w_ffn_gate[l, 1], w_ffn_up[l, 1], w_ffn_down[l, 1])
        x = ffn_half(x, mod[:, 0], g_pre[l, 0], g_post[l, 0], *ffn1)
        ctx = ffn_half(ctx, mod_c[:, 0], g_pre[l, 0], g_post[l, 0], *ffn1)
        u = adaln(x, g_pre[l, 1], mod[:, 1, 0], mod[:, 1, 1])
        u_c = adaln(ctx, g_pre[l, 1], mod_c[:, 1, 0], mod_c[:, 1, 1])
        y, y_c = token_mixers(u, u_c, w_in[l], gqa_q_norm[l], gqa_k_norm[l], mla_q_norm[l], mla_kv_norm[l],
                              mla_w_uq[l], mla_w_ukv[l], diff_lambda[l], diff_subln[l], swa_sink[l],
                              w_branch[l], w_out[l], diff_lambda_init(l), ropes, need_ctx)
        x = x + mod[:, 1, 2][:, None, :] * rms_norm(y, g_post[l, 1])
        x = ffn_half(x, mod[:, 2], g_pre[l, 2], g_post[l, 2], *ffn2)
        if need_ctx:
            ctx = ctx + mod_c[:, 1, 2][:, None, :] * rms_norm(y_c, g_post[l, 1])
            ctx = ffn_half(ctx, mod_c[:, 2], g_pre[l, 2], g_post[l, 2], *ffn2)
    return x
```

```python
import math
import numpy as np
import ml_dtypes
import concourse.bass as bass
import concourse.mybir as mybir
from concourse.bass_utils import run_bass_kernel_spmd

F32 = mybir.dt.float32
BF16 = mybir.dt.bfloat16
U8 = mybir.dt.uint8
AF = mybir.ActivationFunctionType
ALU = mybir.AluOpType

D = 1024
TL = 8192
TC = 256
T = TL + TC
DFF = 2816
NF = DFF // 128
DEPTH = 2
EPS = 1e-6
NV = 56
ENGS = ("pe", "act", "dve", "pool", "sp")
EPOCH = 30000
EW2 = 'dve'


class Tok:
    __slots__ = ("name", "wr", "rd", "dsem", "dcnt", "base", "dq")

    def __init__(self, name=""):
        self.name = name
        self.wr = {}
        self.base = {}
        self.rd = {}
        self.dsem = None
        self.dcnt = 0
        self.dq = None


class _Rec:
    def __init__(self):
        self.call = None

    def __getattr__(self, name):
        def f(*a, **k):
            self.call = (name, a, k)
            return self
        return f


class Sched:
    def __init__(self, nc):
        self.nc = nc
        self.ops = {e: [] for e in ENGS}
        self.cnt = {e: 0 for e in ENGS}
        self.sems = {e: [] for e in ENGS}
        self.waited = {e: {} for e in ENGS}
        self.free_dsems = {'sw': [], 'hw': []}
        self.live_dtoks = []
        self.all_dsems = []
        self.n_ins = 0
        self.n_wait = 0

    def _esem(self, e, epoch):
        while len(self.sems[e]) <= epoch:
            self.sems[e].append(self.nc.alloc_semaphore(f"s_{e}_{len(self.sems[e])}"))
        return self.sems[e][epoch]

    def _dsem(self, tok, q):
        kind = 'sw' if q == 'pool' else 'hw'
        if tok.dsem is None:
            fl = self.free_dsems[kind]
            if fl:
                sem, cnt = fl.pop(0)
            else:
                sem = self.nc.alloc_semaphore(f"d{kind}_{len(self.all_dsems)}")
                self.all_dsems.append(sem)
                cnt = 0
            tok.dsem = sem
            tok.dcnt = cnt
            tok.dq = kind
            self.live_dtoks.append(tok)
        assert tok.dq == kind, f"token {tok.name} used by both SW and HW DMA queues"
        return tok.dsem

    @staticmethod
    def _ekey(e, seq):
        epoch, v = divmod(seq - 1, EPOCH)
        return ('e', e, epoch), v + 1

    def _need(self, e, key, ent, deps):
        if key[0] == 'e':
            val = ent
            sem = self._esem(key[1], key[2])
        else:
            tok = ent
            if tok.dsem is None:
                return
            val = tok.dcnt
            sem = tok.dsem
        if self.waited[e].get(key, 0) >= val:
            return
        if key not in deps or deps[key][1] < val:
            deps[key] = (sem, val)

    def _collect(self, e, reads, writes, pwrites):
        deps = {}
        skip = (e == 'pe')
        for t in reads:
            for dct in (t.base, t.wr):
                for key, ent in dct.items():
                    if key[0] == 'e' and key[1] == e and skip:
                        continue
                    self._need(e, key, ent, deps)
        for t in writes:
            for dct in (t.base, t.wr, t.rd):
                for key, ent in dct.items():
                    if key[0] == 'e' and key[1] == e and skip:
                        continue
                    self._need(e, key, ent, deps)
        for t in pwrites:
            for dct in (t.base, t.rd):
                for key, ent in dct.items():
                    if key[0] == 'e' and key[1] == e and skip:
                        continue
                    self._need(e, key, ent, deps)
        out = []
        for key, (sem, val) in deps.items():
            self.waited[e][key] = val
            out.append((sem, val))
        self.n_wait += len(out)
        return out

    def _record(self, key, ent, reads, writes, pwrites):
        for t in reads:
            if key[0] == 'e':
                if t.rd.get(key, 0) < ent:
                    t.rd[key] = ent
            else:
                t.rd[key] = ent
        for t in writes:
            t.base = {key: ent}
            t.wr = {}
            t.rd = {}
        for t in pwrites:
            if key[0] == 'e':
                if t.wr.get(key, 0) < ent:
                    t.wr[key] = ent
            else:
                t.wr[key] = ent

    def op(self, e, fn, reads=(), writes=(), pwrites=(), inc=True):
        waits = self._collect(e, reads, writes, pwrites)
        if inc:
            self.cnt[e] += 1
            seq = self.cnt[e]
        else:
            seq = self.cnt[e] + 1
        key, val = self._ekey(e, seq)
        sem = self._esem(e, key[2]) if inc else None
        self.n_ins += 1

        rec = _Rec()
        fn(rec)
        call = rec.call

        def emit(eng, waits=waits, call=call, sem=sem):
            for (s, v) in waits:
                eng.wait_ge(s, v)
            ins = getattr(eng, call[0])(*call[1], **call[2])
            if sem is not None:
                ins.then_inc(sem, 1)
        self.ops[e].append(emit)
        self._record(key, val, reads, writes, pwrites)

    def dma(self, q, out, in_, owner, reads=(), writes=(), pwrites=()):
        waits = self._collect(q, reads, writes, pwrites)
        sem = self._dsem(owner, q)
        owner.dcnt += 16
        self.n_ins += 1

        def emit(eng, waits=waits, sem=sem, out=out, in_=in_):
            for (s, v) in waits:
                eng.wait_ge(s, v)
            eng.dma_start(out=out, in_=in_).then_inc(sem, 16)
        self.ops[q].append(emit)
        self._record(('d', id(owner)), owner, reads, writes, pwrites)

    def barrier(self, engines=ENGS):
        targets = []
        for f in ENGS:
            if self.cnt[f] > 0:
                key, val = self._ekey(f, self.cnt[f])
                targets.append((key, self._esem(f, key[2]), val))
        dts = [(('d', id(t)), t.dsem, t.dcnt) for t in self.live_dtoks]
        for e in engines:
            waits = []
            for key, sem, val in targets:
                if key[1] == e:
                    continue
                if self.waited[e].get(key, 0) < val:
                    self.waited[e][key] = val
                    waits.append((sem, val))
            for key, sem, val in dts:
                if self.waited[e].get(key, 0) < val:
                    waits.append((sem, val))

            def emit(eng, waits=waits):
                for (s, v) in waits:
                    eng.wait_ge(s, v)
            self.ops[e].append(emit)
            self.n_wait += len(waits)
        if tuple(engines) == ENGS:
            for t in self.live_dtoks:
                self.free_dsems[t.dq].append((t.dsem, t.dcnt))
                t.dsem = None
                t.wr = {}
                t.rd = {}
                t.base = {}
            self.live_dtoks = []
            for e in ENGS:
                self.waited[e] = {k: v for k, v in self.waited[e].items() if k[0] == 'e'}

    def emit_all(self):
        nc = self.nc
        ops = self.ops
        with nc.Block() as block:
            @block.tensor
            def _(eng):
                for f in ops['pe']:
                    f(eng)

            @block.scalar
            def _(eng):
                for f in ops['act']:
                    f(eng)

            @block.vector
            def _(eng):
                for f in ops['dve']:
                    f(eng)

            @block.gpsimd
            def _(eng):
                for f in ops['pool']:
                    f(eng)

            @block.sync
            def _(eng):
                for f in ops['sp']:
                    f(eng)


class Pool:
    CAP = 212000

    def __init__(self, nc):
        self.t = nc.alloc_sbuf_tensor("sbpool", [128, self.CAP], U8)
        self.off = 0
        self.base = 0

    def set_base(self):
        self.base = self.off

    def reset(self):
        self.off = self.base

    def alloc(self, shape, dtype):
        esz = 4 if dtype == F32 else 2
        n = 1
        for s in shape[1:]:
            n *= s
        nbytes = (n * esz + 63) // 64 * 64
        assert self.off + nbytes <= self.CAP, f"SBUF overflow {self.off}+{nbytes}"
        v = self.t[:, self.off:self.off + n * esz].bitcast(dtype)
        self.off += nbytes
        if len(shape) == 3:
            v = v.rearrange("p (a b) -> p a b", b=shape[2])
        elif len(shape) == 4:
            v = v.rearrange("p (a b c) -> p a b c", b=shape[2], c=shape[3])
        return v


class B:
    def __init__(self, ap, name=""):
        self.ap = ap
        self.tok = Tok(name)


def build_program(dbg=None):
    nc = bass.Bass("TRN2", target_bir_lowering=False)
    S = Sched(nc)
    P = Pool(nc)

    def din(name, shape, dt=F32):
        return nc.dram_tensor(name, list(shape), dt, kind="ExternalInput").ap()

    def dscr(name, shape, dt):
        kind = "ExternalOutput" if (dbg and name in dbg) else "Internal"
        return nc.dram_tensor(name, list(shape), dt, kind=kind).ap()

    xT_in = din("xT", [D, T])
    ccT_d = din("ccT", [128, 2, 8])
    w_mod_d = din("w_mod", [DEPTH, D, 9 * D])
    bmodT_d = din("bmodT", [DEPTH, 128, 72])
    vecT_d = din("vecT", [DEPTH, 128, NV])
    sinkB_d = din("sinkB", [DEPTH, 128, 4])
    lamB_d = din("lamB", [DEPTH, 128, 128])
    wg_d = din("w_ffn_gate", [DEPTH, 2, D, DFF])
    wu_d = din("w_ffn_up", [DEPTH, 2, D, DFF])
    wd_d = din("w_ffn_down", [DEPTH, 2, DFF, D])
    wfm_d = din("w_fm", [DEPTH, D, 1664])
    wfmr_d = din("w_fmr", [DEPTH, D, 1280])
    wkpe_d = din("w_kpe", [DEPTH, D, 64])
    wv_d = din("w_v", [DEPTH, D, 512])
    wgate_d = din("w_gate", [DEPTH, D, 4096])
    wuqn_d = din("wuq_n", [DEPTH, 256, 256])
    wuqp_d = din("wuq_p", [DEPTH, 256, 128])
    wuqpr_d = din("wuq_pr", [DEPTH, 256, 128])
    wukvk_d = din("wukv_k", [DEPTH, 128, 256])
    wukvv_d = din("wukv_v", [DEPTH, 128, 256])
    wbr_d = din("w_branch", [DEPTH, 4, 256, D])
    wout_d = din("w_out", [DEPTH, D, D])
    rope_d = {k: din(k, [128, T]) for k in ("ropeC64", "ropeS64", "ropeC32", "ropeS32")}
    masks_d = din("masks", [128, 3, 128], BF16)
    outT_d = nc.dram_tensor("outT", [D, TL], F32, kind="ExternalOutput").ap()

    xT = dscr("xT_s", [D, T], F32)
    Hs = dscr("H_s", [DFF, T], BF16)
    QA = dscr("QA", [256, T], BF16); KA = dscr("KA", [128, T], BF16)
    QC = dscr("QC", [256, T], BF16); KC = dscr("KC", [256, T], BF16)
    QD = dscr("QD", [256, T], BF16); KD = dscr("KD", [128, T], BF16)
    QBn = dscr("QBn", [256, T], BF16); QBp = dscr("QBp", [128, T], BF16)
    KBn = dscr("KBn", [256, T], BF16); KBp = dscr("KBp", [32, T], BF16)
    VACD = dscr("VACD", [T, 512], BF16); VB = dscr("VB", [T, 256], BF16)
    OT = dscr("OT", [4 * 256, T], BF16)
    tok_attn_in = Tok("attn_in")
    tok_OT = Tok("OT")
    NCH = 17
    xtok = [Tok(f"x{c}") for c in range(NCH)]
    htok = [Tok(f"h{c}") for c in range(NCH)]

    def chunk(ci):
        return (ci * 512, 512 if ci < 16 else 256, 0 if ci < 16 else 1)

    psum = nc.alloc_psum_tensor("psum", [128, 4096], F32)
    bank = [B(psum[:, 512 * i:512 * (i + 1)], f"bank{i}") for i in range(8)]

    ones_bf = B(P.alloc([128, 128], BF16), "ones")
    bd_bf = B(P.alloc([128, 128], BF16), "bd")
    epsc = B(P.alloc([128, 1], F32), "eps")
    masks = B(P.alloc([128, 3, 128], BF16), "masks")
    onesf = B(P.alloc([128, 64], F32), "onesf")
    sinkrow = B(P.alloc([128, 4, 128], BF16), "sinkrow")
    cc = B(P.alloc([128, 2, 8], F32), "cc")
    modT = B(P.alloc([128, 2, 72], F32), "modT")
    vecT = B(P.alloc([128, NV], F32), "vecT")
    Avec = B(P.alloc([128, 2, 24], F32), "Avec")
    Gvec = B(P.alloc([128, 2, 24], F32), "Gvec")
    esink = B(P.alloc([128, 4], F32), "esink")
    lamc = B(P.alloc([128, 4], F32), "lamc")
    subg = B(P.alloc([128, 1], F32), "subg")
    P.set_base()

    S.op('dve', lambda e: e.memset(ones_bf.ap, 1.0), writes=[ones_bf.tok])
    S.op('dve', lambda e: e.memset(bd_bf.ap, 0.0), writes=[bd_bf.tok])
    S.op('dve', lambda e: e.memset(bd_bf.ap[0:64, 0:64], 1.0), pwrites=[bd_bf.tok])
    S.op('dve', lambda e: e.memset(bd_bf.ap[64:128, 64:128], 1.0), pwrites=[bd_bf.tok])
    S.op('dve', lambda e: e.memset(epsc.ap, EPS), writes=[epsc.tok])
    S.op('dve', lambda e: e.memset(onesf.ap, 1.0), writes=[onesf.tok])
    S.op('dve', lambda e: e.memset(sinkrow.ap, 0.0), writes=[sinkrow.tok])
    S.dma('pool', masks.ap, masks_d, masks.tok, writes=[masks.tok])
    S.dma('sp', cc.ap, ccT_d, cc.tok, writes=[cc.tok])
    S.op('act', lambda e: e.activation(out=cc.ap, in_=cc.ap, func=AF.Silu), reads=[cc.tok], writes=[cc.tok])

    def mm_group(out_ap, pairs, reads, out_tok):
        n = len(pairs)
        for i, (l, r) in enumerate(pairs):
            S.op('pe', (lambda e, l=l, r=r, i=i: e.matmul(out_ap, lhsT=l, rhs=r, start=(i == 0), stop=(i == n - 1))),
                 reads=reads, writes=[out_tok], inc=(i == n - 1))

    def rstd_from_ss(ss_bank, N, dst, inv_n, rows=slice(0, 128)):
        S.op('act', lambda e: e.activation(out=dst.ap[rows, :N], in_=ss_bank.ap[rows, :N], func=AF.Ln,
                                           bias=epsc.ap[rows, :], scale=inv_n),
             reads=[ss_bank.tok, epsc.tok], writes=[dst.tok])
        S.op('act', lambda e: e.activation(out=dst.ap[rows, :N], in_=dst.ap[rows, :N], func=AF.Exp, scale=-0.5),
             reads=[dst.tok], writes=[dst.tok])

    def load_w_cast(dst, src, q='pool'):
        if len(dst.ap.shape) == 3:
            for c_ in range(dst.ap.shape[1]):
                S.dma(q, dst.ap[:, c_, :], src[:, c_, :], dst.tok, pwrites=[dst.tok])
        else:
            S.dma(q, dst.ap, src, dst.tok, pwrites=[dst.tok])

    def adaln_sq(xb, ub, N):
        S.op('act', lambda e: e.activation(out=ub.ap[:, :, :N], in_=xb.ap[:, :, :N], func=AF.Square),
             reads=[xb.tok], writes=[ub.tok])

    def adaln_ss(ub, N, ssb):
        mm_group(ssb.ap[:, :N], [(ones_bf.ap, ub.ap[:, c, :N]) for c in range(8)], [ones_bf.tok, ub.tok], ssb.tok)

    def adaln_fin_start(N, ssb, rs):
        rstd_from_ss(ssb, N, rs, 1.0 / D)

    def adaln_fin_c(xb, ub, N, w, s, rs, tmps, c):
        tmp = tmps[c % len(tmps)]
        S.op('dve', lambda e: e.scalar_tensor_tensor(
            out=tmp.ap[:, :N], in0=xb.ap[:, c, :N], scalar=Avec.ap[:, w, s * 8 + c:s * 8 + c + 1],
            in1=rs.ap[:, :N], op0=ALU.mult, op1=ALU.mult),
            reads=[xb.tok, Avec.tok, rs.tok], writes=[tmp.tok])
        S.op('dve', lambda e: e.tensor_scalar(
            out=ub.ap[:, c, :N], in0=tmp.ap[:, :N], scalar1=modT.ap[:, w, s * 24 + c:s * 24 + c + 1], scalar2=None,
            op0=ALU.add),
            reads=[tmp.tok, modT.tok], writes=[ub.tok] if c == 0 else [], pwrites=[] if c == 0 else [ub.tok])

    def adaln_fin(xb, ub, N, w, s, ssb, rs, tmps):
        adaln_fin_start(N, ssb, rs)
        for c in range(8):
            adaln_fin_c(xb, ub, N, w, s, rs, tmps, c)

    def adaln(xb, ub, N, w, s, ssb, rs, tmps):
        adaln_sq(xb, ub, N)
        adaln_ss(ub, N, ssb)
        adaln_fin(xb, ub, N, w, s, ssb, rs, tmps)

    xview = xT.rearrange("(c p) t -> p c t", p=128)

    def load_x(xb, ci, src=None):
        t0, N, w = chunk(ci)
        v = (src if src is not None else xview)
        S.dma('sp', xb.ap[:, :, :N], v[:, :, t0:t0 + N], xb.tok, reads=[xtok[ci]], writes=[xb.tok])

    def phase_mod(l):
        P.reset()
        S.barrier()
        wp = [B(P.alloc([128, 8, 512], F32), f"wmod{i}") for i in range(2)]
        bm = B(P.alloc([128, 72], F32), "bm")
        lam_t = B(P.alloc([128, 128], F32), "lam_t")
        snk = B(P.alloc([128, 4], F32), "snk")
        mps = bank[0]
        S.dma('sp', bm.ap, bmodT_d[l], bm.tok, writes=[bm.tok])
        S.dma('sp', vecT.ap, vecT_d[l], vecT.tok, writes=[vecT.tok])
        S.dma('sp', lam_t.ap, lamB_d[l], lam_t.tok, writes=[lam_t.tok])
        S.dma('sp', snk.ap, sinkB_d[l], snk.tok, writes=[snk.tok])
        wv = w_mod_d[l].rearrange("(c p) j -> p c j", p=128)
        for jp in range(18):
            buf = wp[jp % 2]
            S.dma('sp', buf.ap, wv[:, :, jp * 512:(jp + 1) * 512], buf.tok, writes=[buf.tok])
            for jj in range(4):
                j = jp * 4 + jj
                mm_group(mps.ap[:, 2 * j:2 * j + 2],
                         [(buf.ap[:, kc, jj * 128:(jj + 1) * 128], cc.ap[:, :, kc]) for kc in range(8)],
                         [buf.tok, cc.tok], mps.tok)
        mv = mps.ap[:, 0:144].rearrange("p (j w) -> p j w", w=2)
        for w in range(2):
            S.op('dve', lambda e, w=w: e.tensor_tensor(out=modT.ap[:, w, :], in0=mv[:, :, w], in1=bm.ap, op=ALU.add),
                 reads=[mps.tok, bm.tok], writes=[modT.tok] if w == 0 else [], pwrites=[] if w == 0 else [modT.tok])
        for w in range(2):
            for s in range(3):
                S.op('dve', lambda e, w=w, s=s: e.scalar_tensor_tensor(
                    out=Avec.ap[:, w, s * 8:(s + 1) * 8], in0=modT.ap[:, w, s * 24 + 8:s * 24 + 16], scalar=1.0,
                    in1=vecT.ap[:, s * 8:(s + 1) * 8], op0=ALU.add, op1=ALU.mult),
                    reads=[modT.tok, vecT.tok], pwrites=[Avec.tok])
                S.op('dve', lambda e, w=w, s=s: e.scalar_tensor_tensor(
                    out=Gvec.ap[:, w, s * 8:(s + 1) * 8], in0=modT.ap[:, w, s * 24 + 16:s * 24 + 24],
                    scalar=(1.0 if s == 1 else 0.5),
                    in1=vecT.ap[:, 24 + s * 8:24 + (s + 1) * 8], op0=ALU.mult, op1=ALU.mult),
                    reads=[modT.tok, vecT.tok], pwrites=[Gvec.tok])
        S.op('act', lambda e: e.activation(out=esink.ap, in_=snk.ap, func=AF.Exp), reads=[snk.tok], writes=[esink.tok])
        for h_ in range(4):
            S.op('dve', lambda e, h_=h_: e.tensor_scalar(out=sinkrow.ap[0:1, h_, 64:128], in0=onesf.ap[0:1, 0:64],
                                                        scalar1=esink.ap[0:1, h_:h_ + 1], scalar2=None, op0=ALU.mult),
                 reads=[onesf.tok, esink.tok], writes=[sinkrow.tok] if h_ == 0 else [], pwrites=[] if h_ == 0 else [sinkrow.tok])
        lam_init = 0.8 - 0.6 * math.exp(-0.3 * l)
        pr = B(P.alloc([128, 64], F32), "pr")
        S.op('dve', lambda e: e.tensor_tensor(out=pr.ap[:, 0:32], in0=lam_t.ap[:, 0:32], in1=lam_t.ap[:, 32:64], op=ALU.mult),
             reads=[lam_t.tok], writes=[pr.tok])
        S.op('dve', lambda e: e.tensor_tensor(out=pr.ap[:, 32:64], in0=lam_t.ap[:, 64:96], in1=lam_t.ap[:, 96:128], op=ALU.mult),
             reads=[lam_t.tok], pwrites=[pr.tok])
        S.op('dve', lambda e: e.tensor_reduce(out=lamc.ap[:, 1:3], in_=pr.ap.rearrange("p (a b) -> p a b", b=32),
                                              axis=mybir.AxisListType.X, op=ALU.add),
             reads=[pr.tok], writes=[lamc.tok])
        S.op('act', lambda e: e.activation(out=lamc.ap[:, 1:3], in_=lamc.ap[:, 1:3], func=AF.Exp), reads=[lamc.tok], writes=[lamc.tok])
        S.op('dve', lambda e: e.scalar_tensor_tensor(out=lamc.ap[:, 0:1], in0=lamc.ap[:, 2:3], scalar=-lam_init,
                                                     in1=lamc.ap[:, 1:2], op0=ALU.add, op1=ALU.subtract),
             reads=[lamc.tok], writes=[lamc.tok])
        S.op('dve', lambda e: e.tensor_scalar(out=subg.ap, in0=vecT.ap[:, 55:56], scalar1=(1.0 - lam_init), scalar2=None,
                                              op0=ALU.mult), reads=[vecT.tok], writes=[subg.tok])

    def phase_ffn_a(l, i, s, chunks, src_first=None):
        P.reset()
        S.barrier()
        fblk = [(0, 2), (2, 6), (6, 14), (14, NF)]
        Wg_ap = P.alloc([128, 8, DFF], BF16)
        Wu_ap = P.alloc([128, 8, DFF], BF16)
        Wgb = [B(Wg_ap[:, :, a_ * 128:b_ * 128], f"Wg{a_}") for (a_, b_) in fblk]
        Wub = [B(Wu_ap[:, :, a_ * 128:b_ * 128], f"Wu{a_}") for (a_, b_) in fblk]
        fb_of = {}
        for bi_, (a_, b_) in enumerate(fblk):
            for f_ in range(a_, b_):
                fb_of[f_] = (bi_, (f_ - a_) * 128)
            for Wb_, src_ in ((Wgb[bi_], wg_d), (Wub[bi_], wu_d)):
                for kc in range(8):
                    S.dma('pool', Wb_.ap[:, kc, :], src_[l, i, kc * 128:(kc + 1) * 128, a_ * 128:b_ * 128], Wb_.tok,
                          pwrites=[Wb_.tok])
        xb = [B(P.alloc([128, 8, 512], F32), f"xb{k}") for k in range(2)]
        ub = [B(P.alloc([128, 8, 512], BF16), f"ub{k}") for k in range(2)]
        hst = [B(P.alloc([128, NF, 512], BF16), f"hst{k}") for k in range(2)]
        rs = [B(P.alloc([128, 512], F32), f"rs{k}") for k in range(2)]
        tmps = [B(P.alloc([128, 512], F32), f"tmp{k}") for k in range(2)]
        sg = [B(P.alloc([128, 512], F32), f"sg{k}") for k in range(2)]
        tmps2 = [B(P.alloc([128, 512], F32), f"tmpb{k}") for k in range(2)]
        ssb = bank[0]
        gb = [bank[1], bank[2]]
        upb = [bank[3], bank[4]]
        hview = Hs.rearrange("(f p) t -> p f t", p=128)
        load_x(xb[0], chunks[0], src_first)
        for n, ci in enumerate(chunks):
            t0, N, w = chunk(ci)
            k = n % 2
            if n + 1 < len(chunks):
                load_x(xb[(n + 1) % 2], chunks[n + 1], src_first)
            if n == 0:
                adaln(xb[k], ub[k], N, w, s, ssb, rs[k], tmps)
            for f in range(NF):
                if n + 1 < len(chunks) and (f in (4, 7, 11) or 12 <= f < 20):
                    t0n, Nn, wn = chunk(chunks[n + 1])
                    k1 = (n + 1) % 2
                    if f == 4:
                        adaln_sq(xb[k1], ub[k1], Nn)
                    elif f == 7:
                        adaln_ss(ub[k1], Nn, ssb)
                    elif f == 11:
                        adaln_fin_start(Nn, ssb, rs[k1])
                    else:
                        adaln_fin_c(xb[k1], ub[k1], Nn, wn, s, rs[k1], tmps2, f - 12)
                g = gb[f % 2]
                up = upb[f % 2]
                bi_, c0_ = fb_of[f]
                Wg, Wu = Wgb[bi_], Wub[bi_]
                mm_group(g.ap[:, :N], [(Wg.ap[:, kc, c0_:c0_ + 128], ub[k].ap[:, kc, :N]) for kc in range(8)],
                         [Wg.tok, ub[k].tok], g.tok)
                mm_group(up.ap[:, :N], [(Wu.ap[:, kc, c0_:c0_ + 128], ub[k].ap[:, kc, :N]) for kc in range(8)],
                         [Wu.tok, ub[k].tok], up.tok)
                sgt = sg[f % 2]
                S.op('act', lambda e, g=g, sgt=sgt: e.activation(out=sgt.ap[:, :N], in_=g.ap[:, :N], func=AF.Silu),
                     reads=[g.tok], writes=[sgt.tok])
                S.op('dve', lambda e, f=f, up=up, sgt=sgt: e.tensor_tensor(out=hst[k].ap[:, f, :N], in0=up.ap[:, :N],
                                                                         in1=sgt.ap[:, :N], op=ALU.mult),
                     reads=[up.tok, sgt.tok], writes=[hst[k].tok] if f == 0 else [], pwrites=[] if f == 0 else [hst[k].tok])
            S.dma('sp', hview[:, :, t0:t0 + N], hst[k].ap[:, :, :N], hst[k].tok, reads=[hst[k].tok], writes=[htok[ci]])

    def phase_ffn_b(l, i, s, chunks, src_first=None, final_out=False):
        P.reset()
        S.barrier()
        dblk = [(0, 2), (2, 4), (4, 8)]
        Wd_ap = P.alloc([128, NF, D], BF16)
        Wdb = [B(Wd_ap[:, :, a_ * 128:b_ * 128], f"Wd{a_}") for (a_, b_) in dblk]
        db_of = {}
        for bi_, (a_, b_) in enumerate(dblk):
            for d_ in range(a_, b_):
                db_of[d_] = (bi_, (d_ - a_) * 128)
            for f in range(NF):
                S.dma('pool', Wdb[bi_].ap[:, f, :], wd_d[l, i, f * 128:(f + 1) * 128, a_ * 128:b_ * 128], Wdb[bi_].tok,
                      pwrites=[Wdb[bi_].tok])
        xb = [B(P.alloc([128, 8, 512], F32), f"xb{k}") for k in range(2)]
        hb = [B(P.alloc([128, NF, 512], BF16), f"hb{k}") for k in range(2)]
        yb = B(P.alloc([128, 8, 512], F32), "yb")
        ysq = B(P.alloc([128, 8, 512], BF16), "ysq")
        rs = B(P.alloc([128, 512], F32), "rs")
        tmps = [B(P.alloc([128, 512], F32), f"tmp{k}") for k in range(2)]
        ypb = [bank[1], bank[2], bank[3]]
        ssb = bank[0]
        hview = Hs.rearrange("(f p) t -> p f t", p=128)
        oview = outT_d.rearrange("(c p) t -> p c t", p=128)

        def loads(n):
            ci = chunks[n]
            t0, N, w = chunk(ci)
            S.dma('sp', hb[n % 2].ap[:, :, :N], hview[:, :, t0:t0 + N], hb[n % 2].tok, reads=[htok[ci]], writes=[hb[n % 2].tok])
            load_x(xb[n % 2], ci, src_first)
        loads(0)
        for n, ci in enumerate(chunks):
            t0, N, w = chunk(ci)
            k = n % 2
            if n + 1 < len(chunks):
                loads(n + 1)
            lvl = dbg.get("lvl", 9) if dbg else 9
            for d in range(8 if lvl >= 2 else 0):
                yp = ypb[d % 3]
                bi_, c0_ = db_of[d]
                Wd = Wdb[bi_]
                mm_group(yp.ap[:, :N], [(Wd.ap[:, f, c0_:c0_ + 128], hb[k].ap[:, f, :N]) for f in range(NF)],
                         [Wd.tok, hb[k].tok], yp.tok)
                var = dbg.get("var", "ab") if dbg else "ab"
                if "a" in var:
                    S.op('act', lambda e, d=d, yp=yp: e.activation(out=ysq.ap[:, d, :N], in_=yp.ap[:, :N], func=AF.Square),
                         reads=[yp.tok], writes=[ysq.tok] if d == 0 else [], pwrites=[] if d == 0 else [ysq.tok])
                if "b" in var:
                    S.op('dve', lambda e, d=d, yp=yp: e.tensor_copy(out=yb.ap[:, d, :N], in_=yp.ap[:, :N]),
                         reads=[yp.tok, ysq.tok], writes=[yb.tok] if d == 0 else [], pwrites=[] if d == 0 else [yb.tok])
            if lvl >= 3:
                mm_group(ssb.ap[:, :N], [(ones_bf.ap, ysq.ap[:, c, :N]) for c in range(8)], [ones_bf.tok, ysq.tok], ssb.tok)
                rstd_from_ss(ssb, N, rs, 1.0 / D)
            for d in range(8 if lvl >= 4 else 0):
                tmp = tmps[d % 2]
                S.op('dve', lambda e, d=d, tmp=tmp: e.scalar_tensor_tensor(
                    out=tmp.ap[:, :N], in0=yb.ap[:, d, :N], scalar=Gvec.ap[:, w, s * 8 + d:s * 8 + d + 1],
                    in1=rs.ap[:, :N], op0=ALU.mult, op1=ALU.mult),
                    reads=[yb.tok, Gvec.tok, rs.tok], writes=[tmp.tok])
                S.op(EW2, lambda e, d=d, tmp=tmp: e.tensor_tensor(out=xb[k].ap[:, d, :N], in0=xb[k].ap[:, d, :N],
                                                                    in1=tmp.ap[:, :N], op=ALU.add),
                     reads=[tmp.tok, xb[k].tok], writes=[xb[k].tok])
            if final_out:
                S.dma('sp', oview[:, :, t0:t0 + N], xb[k].ap[:, :, :N], xb[k].tok, reads=[xb[k].tok], writes=[xtok[ci]])
            else:
                S.dma('sp', xview[:, :, t0:t0 + N], xb[k].ap[:, :, :N], xb[k].tok, reads=[xb[k].tok], writes=[xtok[ci]])

    def phase_proj(l, chunks):
        P.reset()
        S.barrier()
        Wfm = B(P.alloc([128, 8, 1664], BF16), "Wfm")
        Wfr = B(P.alloc([128, 8, 1280], BF16), "Wfr")
        Wkp = B(P.alloc([128, 8, 64], BF16), "Wkp")
        Wv = B(P.alloc([128, 8, 512], BF16), "Wv")
        Wqn = B(P.alloc([128, 2, 256], BF16), "Wqn")
        Wqp = B(P.alloc([128, 2, 128], BF16), "Wqp")
        Wqr = B(P.alloc([128, 2, 128], BF16), "Wqr")
        Wkk = B(P.alloc([128, 256], BF16), "Wkk")
        Wkv = B(P.alloc([128, 256], BF16), "Wkv")
        for Wt, src in ((Wfm, wfm_d), (Wfr, wfmr_d), (Wkp, wkpe_d), (Wv, wv_d)):
            load_w_cast(Wt, src[l].rearrange("(c p) j -> p c j", p=128))
        for Wt, src in ((Wqn, wuqn_d), (Wqp, wuqp_d), (Wqr, wuqpr_d)):
            load_w_cast(Wt, src[l].rearrange("(c p) j -> p c j", p=128))
        load_w_cast(Wkk, wukvk_d[l])
        load_w_cast(Wkv, wukvv_d[l])
        xb = [B(P.alloc([128, 8, 512], F32), f"xb{k}") for k in range(2)]
        ubs = [B(P.alloc([128, 8, 512], BF16), f"ub{k}") for k in range(2)]
        rss = [B(P.alloc([128, 512], F32), f"rs{k}") for k in range(2)]
        tmps = [B(P.alloc([128, 512], F32), f"tmp{k}") for k in range(2)]
        tabs = [{k: B(P.alloc([128, 512], F32), f"{k}{j}") for k in rope_d} for j in range(2)]
        NST = 6
        stg = [B(P.alloc([128, 512], BF16), f"stg{j}") for j in range(NST)]
        stg_i = [0]
        t1s = [B(P.alloc([128, 512], F32), f"t1_{j}") for j in range(2)]
        t2s = [B(P.alloc([128, 512], F32), f"t2_{j}") for j in range(2)]
        sqa = B(P.alloc([128, 512], BF16), "sqa")
        rsa = B(P.alloc([128, 512], F32), "rsa")
        cqn = B(P.alloc([128, 2, 512], BF16), "cqn")
        cqsq = B(P.alloc([128, 2, 512], BF16), "cqsq")
        ckvn = B(P.alloc([128, 512], BF16), "ckvn")
        vst = [B(P.alloc([128, 4, 512], BF16), f"vst{j}") for j in range(2)]
        vbst = [B(P.alloc([128, 4, 256], BF16), f"vbst{j}") for j in range(2)]
        ssb = bank[0]
        ssb2 = bank[7]
        p1b = [bank[1], bank[2]]
        p2b = [bank[3], bank[4]]
        pxb = [bank[5], bank[6]]
        gi = [0]

        def next_stg():
            b_ = stg[stg_i[0] % NST]
            stg_i[0] += 1
            return b_

        def store(dst_ap, st, rows, N):
            S.dma('sp', dst_ap, st.ap[rows, :N], st.tok, reads=[st.tok], pwrites=[tok_attn_in])

        def loads(n):
            ci = chunks[n]
            t0, N, w = chunk(ci)
            load_x(xb[n % 2], ci)
            for kk, src in rope_d.items():
                tb = tabs[n % 2][kk]
                S.dma('sp', tb.ap[:, :N], src[:, t0:t0 + N], tb.tok, writes=[tb.tok])
        loads(0)
        for n_, ci in enumerate(chunks):
            t0, N, w = chunk(ci)
            k = n_ % 2
            if n_ + 1 < len(chunks):
                loads(n_ + 1)
            tb = tabs[k]
            ub = ubs[k]
            if n_ == 0:
                adaln(xb[k], ub, N, w, 1, ssb2, rss[k], tmps)
            groups = [(0, 'A', QA, 0), (1, 'A', QA, 128), (2, 'A', KA, 0),
                      (3, 'C', QC, 0), (4, 'C', QC, 128), (5, 'C', KC, 0), (6, 'C', KC, 128),
                      (7, 'D', QD, 0), (8, 'D', QD, 128), (9, 'D', KD, 0)]
            for (g, kind, dst, r0) in groups:
                if n_ + 1 < len(chunks):
                    t0n, Nn, wn = chunk(chunks[n_ + 1])
                    k1 = (n_ + 1) % 2
                    if g == 1:
                        adaln_sq(xb[k1], ubs[k1], Nn)
                    elif g == 3:
                        adaln_ss(ubs[k1], Nn, ssb2)
                    elif g == 5:
                        adaln_fin_start(Nn, ssb2, rss[k1])
                    elif g >= 6:
                        for c_ in ((g - 6) * 2, (g - 6) * 2 + 1):
                            adaln_fin_c(xb[k1], ubs[k1], Nn, wn, 1, rss[k1], tmps, c_)
                p1 = p1b[gi[0] % 2]
                p2 = p2b[gi[0] % 2]
                t1 = t1s[gi[0] % 2]
                t2 = t2s[gi[0] % 2]
                gi[0] += 1
                mm_group(p1.ap[:, :N], [(Wfm.ap[:, kc, g * 128:(g + 1) * 128], ub.ap[:, kc, :N]) for kc in range(8)],
                         [Wfm.tok, ub.tok], p1.tok)
                mm_group(p2.ap[:, :N], [(Wfr.ap[:, kc, g * 128:(g + 1) * 128], ub.ap[:, kc, :N]) for kc in range(8)],
                         [Wfr.tok, ub.tok], p2.tok)
                st = next_stg()
                if kind == 'A':
                    Ct, St = tb["ropeC64"], tb["ropeS64"]
                    gcol = 48 if g < 2 else 50
                    S.op('act', lambda e, p1=p1: e.activation(out=sqa.ap[:, :N], in_=p1.ap[:, :N], func=AF.Square),
                         reads=[p1.tok], writes=[sqa.tok])
                    mm_group(ssb.ap[:, :N], [(bd_bf.ap, sqa.ap[:, :N])], [bd_bf.tok, sqa.tok], ssb.tok)
                    rstd_from_ss(ssb, N, rsa, 1.0 / 64)
                    S.op('dve', lambda e, p1=p1, t1=t1, Ct=Ct, gcol=gcol: e.scalar_tensor_tensor(
                        out=t1.ap[:, :N], in0=p1.ap[:, :N], scalar=vecT.ap[:, gcol:gcol + 1], in1=Ct.ap[:, :N],
                        op0=ALU.mult, op1=ALU.mult), reads=[p1.tok, vecT.tok, Ct.tok, sqa.tok], writes=[t1.tok])
                    S.op('dve', lambda e, p2=p2, t2=t2, St=St, gcol=gcol: e.scalar_tensor_tensor(
                        out=t2.ap[:, :N], in0=p2.ap[:, :N], scalar=vecT.ap[:, gcol + 1:gcol + 2], in1=St.ap[:, :N],
                        op0=ALU.mult, op1=ALU.mult), reads=[p2.tok, vecT.tok, St.tok], writes=[t2.tok])
                    S.op(EW2, lambda e, t1=t1, t2=t2: e.tensor_tensor(out=t1.ap[:, :N], in0=t1.ap[:, :N], in1=t2.ap[:, :N],
                                                                        op=ALU.add), reads=[t1.tok, t2.tok], writes=[t1.tok])
                    S.op(EW2, lambda e, t1=t1, st=st: e.tensor_tensor(out=st.ap[:, :N], in0=t1.ap[:, :N], in1=rsa.ap[:, :N],
                                                                        op=ALU.mult), reads=[t1.tok, rsa.tok], writes=[st.tok])
                else:
                    if kind == 'C':
                        Ct, St = tb["ropeC32"], tb["ropeS32"]
                    else:
                        Ct, St = tb["ropeC64"], tb["ropeS64"]
                    S.op('dve', lambda e, p1=p1, t1=t1, Ct=Ct: e.tensor_tensor(out=t1.ap[:, :N], in0=p1.ap[:, :N], in1=Ct.ap[:, :N],
                                                                              op=ALU.mult), reads=[p1.tok, Ct.tok], writes=[t1.tok])
                    S.op('dve', lambda e, p2=p2, t2=t2, St=St: e.tensor_tensor(out=t2.ap[:, :N], in0=p2.ap[:, :N], in1=St.ap[:, :N],
                                                                              op=ALU.mult), reads=[p2.tok, St.tok], writes=[t2.tok])
                    S.op(EW2, lambda e, t1=t1, t2=t2, st=st: e.tensor_tensor(out=st.ap[:, :N], in0=t1.ap[:, :N], in1=t2.ap[:, :N],
                                                                               op=ALU.add), reads=[t1.tok, t2.tok], writes=[st.tok])
                store(dst[r0:r0 + 128, t0:t0 + N], st, slice(0, 128), N)
            vs = vst[k]

            def v_tile(j):
                if j >= N // 128:
                    return
                px = pxb[j % 2]
                mm_group(px.ap[:, :512], [(ub.ap[:, kc, j * 128:(j + 1) * 128], Wv.ap[:, kc, :]) for kc in range(8)],
                         [Wv.tok, ub.tok], px.tok)
                S.op('act', lambda e: e.activation(out=vs.ap[:, j, :], in_=px.ap[:, :512], func=AF.Copy),
                     reads=[px.tok], writes=[vs.tok] if j == 0 else [], pwrites=[] if j == 0 else [vs.tok])
            pq = [p1b[0], p1b[1]]
            for c2 in range(2):
                mm_group(pq[c2].ap[:, :N], [(Wfm.ap[:, kc, (10 + c2) * 128:(11 + c2) * 128], ub.ap[:, kc, :N]) for kc in range(8)],
                         [Wfm.tok, ub.tok], pq[c2].tok)
                S.op('act', lambda e, c2=c2: e.activation(out=cqsq.ap[:, c2, :N], in_=pq[c2].ap[:, :N], func=AF.Square),
                     reads=[pq[c2].tok], writes=[cqsq.tok] if c2 == 0 else [], pwrites=[] if c2 == 0 else [cqsq.tok])
            v_tile(0)
            mm_group(ssb.ap[:, :N], [(ones_bf.ap, cqsq.ap[:, c2, :N]) for c2 in range(2)], [ones_bf.tok, cqsq.tok], ssb.tok)
            rstd_from_ss(ssb, N, rsa, 1.0 / 256)
            v_tile(1)
            for c2 in range(2):
                S.op('dve', lambda e, c2=c2: e.scalar_tensor_tensor(
                    out=cqn.ap[:, c2, :N], in0=pq[c2].ap[:, :N], scalar=vecT.ap[:, 52 + c2:53 + c2], in1=rsa.ap[:, :N],
                    op0=ALU.mult, op1=ALU.mult), reads=[pq[c2].tok, vecT.tok, rsa.tok],
                    writes=[cqn.tok] if c2 == 0 else [], pwrites=[] if c2 == 0 else [cqn.tok])
            for g2 in range(2):
                px = pxb[g2]
                mm_group(px.ap[:, :N], [(Wqn.ap[:, kc, g2 * 128:(g2 + 1) * 128], cqn.ap[:, kc, :N]) for kc in range(2)],
                         [Wqn.tok, cqn.tok], px.tok)
                st = next_stg()
                S.op('act', lambda e, px=px, st=st: e.activation(out=st.ap[:, :N], in_=px.ap[:, :N], func=AF.Copy),
                     reads=[px.tok], writes=[st.tok])
                store(QBn[g2 * 128:(g2 + 1) * 128, t0:t0 + N], st, slice(0, 128), N)
            p1, p2 = p2b[0], p2b[1]
            mm_group(p1.ap[:, :N], [(Wqp.ap[:, kc, :], cqn.ap[:, kc, :N]) for kc in range(2)], [Wqp.tok, cqn.tok], p1.tok)
            mm_group(p2.ap[:, :N], [(Wqr.ap[:, kc, :], cqn.ap[:, kc, :N]) for kc in range(2)], [Wqr.tok, cqn.tok], p2.tok)
            t1, t2 = t1s[0], t2s[0]
            st = next_stg()
            Ct, St = tb["ropeC32"], tb["ropeS32"]
            S.op('dve', lambda e: e.tensor_tensor(out=t1.ap[:, :N], in0=p1.ap[:, :N], in1=Ct.ap[:, :N], op=ALU.mult),
                 reads=[p1.tok, Ct.tok], writes=[t1.tok])
            S.op('dve', lambda e: e.tensor_tensor(out=t2.ap[:, :N], in0=p2.ap[:, :N], in1=St.ap[:, :N], op=ALU.mult),
                 reads=[p2.tok, St.tok], writes=[t2.tok])
            S.op(EW2, lambda e, st=st: e.tensor_tensor(out=st.ap[:, :N], in0=t1.ap[:, :N], in1=t2.ap[:, :N], op=ALU.add),
                 reads=[t1.tok, t2.tok], writes=[st.tok])
            store(QBp[:, t0:t0 + N], st, slice(0, 128), N)
            pk = p1b[0]
            mm_group(pk.ap[:, :N], [(Wfm.ap[:, kc, 12 * 128:13 * 128], ub.ap[:, kc, :N]) for kc in range(8)],
                     [Wfm.tok, ub.tok], pk.tok)
            S.op('act', lambda e: e.activation(out=sqa.ap[:, :N], in_=pk.ap[:, :N], func=AF.Square),
                 reads=[pk.tok], writes=[sqa.tok])
            v_tile(2)
            mm_group(ssb.ap[:, :N], [(ones_bf.ap, sqa.ap[:, :N])], [ones_bf.tok, sqa.tok], ssb.tok)
            rstd_from_ss(ssb, N, rsa, 1.0 / 128)
            v_tile(3)
            S.op('dve', lambda e: e.scalar_tensor_tensor(
                out=ckvn.ap[:, :N], in0=pk.ap[:, :N], scalar=vecT.ap[:, 54:55], in1=rsa.ap[:, :N],
                op0=ALU.mult, op1=ALU.mult), reads=[pk.tok, vecT.tok, rsa.tok], writes=[ckvn.tok])
            for g2 in range(2):
                px = pxb[g2]
                mm_group(px.ap[:, :N], [(Wkk.ap[:, g2 * 128:(g2 + 1) * 128], ckvn.ap[:, :N])], [Wkk.tok, ckvn.tok], px.tok)
                st = next_stg()
                S.op('act', lambda e, px=px, st=st: e.activation(out=st.ap[:, :N], in_=px.ap[:, :N], func=AF.Copy),
                     reads=[px.tok], writes=[st.tok])
                store(KBn[g2 * 128:(g2 + 1) * 128, t0:t0 + N], st, slice(0, 128), N)
            p1, p2 = p2b[0], p2b[1]
            mm_group(p1.ap[0:32, :N], [(Wkp.ap[:, kc, 0:32], ub.ap[:, kc, :N]) for kc in range(8)], [Wkp.tok, ub.tok], p1.tok)
            mm_group(p2.ap[0:32, :N], [(Wkp.ap[:, kc, 32:64], ub.ap[:, kc, :N]) for kc in range(8)], [Wkp.tok, ub.tok], p2.tok)
            t1, t2 = t1s[1], t2s[1]
            st = next_stg()
            S.op('dve', lambda e: e.tensor_tensor(out=t1.ap[0:32, :N], in0=p1.ap[0:32, :N], in1=Ct.ap[0:32, :N], op=ALU.mult),
                 reads=[p1.tok, Ct.tok], writes=[t1.tok])
            S.op('dve', lambda e: e.tensor_tensor(out=t2.ap[0:32, :N], in0=p2.ap[0:32, :N], in1=St.ap[0:32, :N], op=ALU.mult),
                 reads=[p2.tok, St.tok], writes=[t2.tok])
            S.op(EW2, lambda e, st=st: e.tensor_tensor(out=st.ap[0:32, :N], in0=t1.ap[0:32, :N], in1=t2.ap[0:32, :N], op=ALU.add),
                 reads=[t1.tok, t2.tok], writes=[st.tok])
            store(KBp[:, t0:t0 + N], st, slice(0, 32), N)
            vb = vbst[k]
            for j in range(N // 128):
                py = p1b[j % 2]
                mm_group(py.ap[:, :256], [(ckvn.ap[:, j * 128:(j + 1) * 128], Wkv.ap[:, :])], [Wkv.tok, ckvn.tok], py.tok)
                S.op('dve', lambda e, py=py, j=j: e.tensor_copy(out=vb.ap[:, j, :], in_=py.ap[:, :256]),
                     reads=[py.tok], writes=[vb.tok] if j == 0 else [], pwrites=[] if j == 0 else [vb.tok])
            nj = N // 128
            S.dma('sp', VACD[t0:t0 + N, :].rearrange("(j p) c -> p j c", p=128), vs.ap[:, :nj, :], vs.tok,
                  reads=[vs.tok], pwrites=[tok_attn_in])
            S.dma('sp', VB[t0:t0 + N, :].rearrange("(j p) c -> p j c", p=128), vb.ap[:, :nj, :], vb.tok,
                  reads=[vb.tok], pwrites=[tok_attn_in])

    def phase_attn(l, need_ctx):
        P.reset()
        S.barrier()
        NKB = T // 128
        sets = []
        for j in range(2):
            sets.append(dict(K=B(P.alloc([128, T], BF16), f"K{j}"), Q=B(P.alloc([128, 2, T], BF16), f"Q{j}"),
                             V=B(P.alloc([128, NKB, 128], BF16), f"V{j}")))
        for j in range(2):
            S.op(EW2, lambda e, j=j: e.memset(sets[j]["V"].ap[:, :, 64:128], 1.0), pwrites=[sets[j]["V"].tok])
        PT = [B(P.alloc([128, 1536], BF16), f"PT{j}") for j in range(3)]
        rz = [B(P.alloc([128, 512], F32), f"rz{j}") for j in range(2)]
        ost = [B(P.alloc([128, 512], BF16), f"ost{j}") for j in range(2)]
        o1 = B(P.alloc([128, 256], F32), "o1")
        o2 = B(P.alloc([128, 256], F32), "o2")
        osq = B(P.alloc([128, 256], BF16), "osq")
        rsd = B(P.alloc([128, 256], F32), "rsd")
        STb = [B(psum[:, 0:1536], "ST0"), B(psum[:, 1536:3072], "ST1")]
        Ob = [bank[6], bank[7]]
        cnt = dict(pt=0, st=0, o=0, rz=0)

        units = []
        for kv in range(2):
            units.append(dict(kind='dense', G=2, dk=64, scale=0.125,
                              kparts=[(KA[64 * kv:64 * kv + 64, :], 0, 64)],
                              qparts=[(QA[(2 * kv + g) * 64:(2 * kv + g + 1) * 64, :], g, 0, 64) for g in range(2)],
                              V=VACD[:, 64 * kv:64 * kv + 64], orow=[(2 * kv + g) * 64 for g in range(2)]))
        for h in range(4):
            units.append(dict(kind='dense', G=1, dk=96, scale=96 ** -0.5,
                              kparts=[(KBn[64 * h:64 * h + 64, :], 0, 64), (KBp[0:32, :], 64, 32)],
                              qparts=[(QBn[64 * h:64 * h + 64, :], 0, 0, 64), (QBp[32 * h:32 * h + 32, :], 0, 64, 32)],
                              V=VB[:, 64 * h:64 * h + 64], orow=[256 + 64 * h]))
        for h in range(4):
            units.append(dict(kind='diff', G=1, dk=64, scale=32 ** -0.5,
                              kparts=[(KC[64 * h:64 * h + 64, :], 0, 64)],
                              qparts=[(QC[64 * h:64 * h + 32, :], 0, 0, 32), (QC[64 * h + 32:64 * h + 64, :], 1, 32, 32)],
                              V=VACD[:, 128 + 64 * h:128 + 64 * h + 64], orow=[512 + 64 * h]))
        for kv in range(2):
            units.append(dict(kind='win', G=2, dk=64, scale=0.125,
                              kparts=[(KD[64 * kv:64 * kv + 64, :], 0, 64)],
                              qparts=[(QD[(2 * kv + g) * 64:(2 * kv + g + 1) * 64, :], g, 0, 64) for g in range(2)],
                              V=VACD[:, 384 + 64 * kv:384 + 64 * kv + 64], orow=[768 + (2 * kv + g) * 64 for g in range(2)],
                              heads=[2 * kv, 2 * kv + 1]))

        def load_unit(u, st):
            K, Q, V = st["K"], st["Q"], st["V"]
            dk_ = u["dk"]
            if dk_ == 64:
                if u["kind"] == 'diff':
                    S.op('dve', lambda e: e.memset(Q.ap[:, :, :], 0.0), writes=[Q.tok])
                for half in (0, 64):
                    for (src, p0, n) in u["kparts"]:
                        S.dma('sp', K.ap[half + p0:half + p0 + n, :], src, K.tok, reads=[tok_attn_in], pwrites=[K.tok])
                    for (src, g, p0, n) in u["qparts"]:
                        S.dma('sp', Q.ap[half + p0:half + p0 + n, g, :], src, Q.tok, reads=[tok_attn_in], pwrites=[Q.tok])
            else:
                S.op('dve', lambda e: e.memset(K.ap[64:128, :], 0.0), writes=[K.tok])
                S.op('dve', lambda e: e.memset(Q.ap[64:128, :, :], 0.0), writes=[Q.tok])
                for (src, p0, n) in u["kparts"]:
                    S.dma('sp', K.ap[p0:p0 + n, :], src, K.tok, reads=[tok_attn_in], pwrites=[K.tok])
                for (src, g, p0, n) in u["qparts"]:
                    S.dma('sp', Q.ap[p0:p0 + n, g, :], src, Q.tok, reads=[tok_attn_in], pwrites=[Q.tok])
            S.dma('sp', V.ap[:, :, 0:64], u["V"].rearrange("(kb p) c -> p kb c", p=128), V.tok,
                  reads=[tok_attn_in], pwrites=[V.tok])

        def free_unit(st):
            pass

        def dense_unit(u, st, qchunks):
            K, Q, V = st["K"], st["Q"], st["V"]
            G, kind = u["G"], u["kind"]
            two = (G == 2 or kind == 'diff')
            items = []
            for (q0, nq, kbs) in qchunks:
                pairs = [kbs[i:i + 3] for i in range(0, len(kbs), 3)]
                for pi, pair in enumerate(pairs):
                    items.append((q0, nq, pair, pi == 0, pi == len(pairs) - 1))
            LOOK = 1
            tiled = (u["dk"] == 64)

            def qk(it):
                q0, nq, pair, first, last = it
                W = nq * (2 if two else 1)
                STt = STb[cnt['st'] % 2]
                cnt['st'] += 1
                for jj, kb in enumerate(pair):
                    base = jj * 512
                    rows = slice(64 * (jj % 2), 64 * (jj % 2) + 64) if tiled else slice(0, 128)
                    if two:
                        out = STt.ap[:, base:base + W].rearrange("p (g q) -> p g q", g=2)
                        rhs = Q.ap[rows, :, q0:q0 + nq]
                    else:
                        out = STt.ap[:, base:base + W]
                        rhs = Q.ap[rows, 0, q0:q0 + nq]
                    S.op('pe', lambda e, out=out, kb=kb, rhs=rhs, rows=rows: e.matmul(
                        out, lhsT=K.ap[rows, kb * 128:(kb + 1) * 128], rhs=rhs, start=True, stop=True),
                        reads=[K.tok, Q.tok], writes=[STt.tok] if jj == 0 else [], pwrites=[] if jj == 0 else [STt.tok],
                        inc=(jj == len(pair) - 1))
                return STt

            def ex(STt, it):
                q0, nq, pair, first, last = it
                W = nq * (2 if two else 1)
                PTt = PT[cnt['pt'] % 3]
                cnt['pt'] += 1
                npair = len(pair)
                if W == 512:
                    S.op('act', lambda e: e.activation(out=PTt.ap[:, :512 * npair], in_=STt.ap[:, :512 * npair], func=AF.Exp,
                                                       scale=u["scale"]), reads=[STt.tok], writes=[PTt.tok])
                else:
                    iv = STt.ap.rearrange("p (j c) -> p j c", c=512)[:, :npair, :W]
                    ov = PTt.ap.rearrange("p (j c) -> p j c", c=512)[:, :npair, :W]
                    S.op('act', lambda e: e.activation(out=ov, in_=iv, func=AF.Exp, scale=u["scale"]),
                         reads=[STt.tok], writes=[PTt.tok])
                return PTt

            def pv(PTt, it, Oc):
                q0, nq, pair, first, last = it
                W = nq * (2 if two else 1)
                for jj, kb in enumerate(pair):
                    st_ = first and jj == 0
                    sp_ = last and jj == len(pair) - 1
                    S.op('pe', lambda e, kb=kb, jj=jj, st_=st_, sp_=sp_: e.matmul(
                        Oc.ap[:, :W], lhsT=V.ap[:, kb, :], rhs=PTt.ap[:, jj * 512:jj * 512 + W], start=st_, stop=sp_),
                        reads=[V.tok, PTt.tok], writes=[Oc.tok] if st_ else [], pwrites=[] if st_ else [Oc.tok], inc=True)

            def normalise(it, Oc):
                q0, nq, pair, first, last = it
                W = nq * (2 if two else 1)
                r = rz[cnt['rz'] % 2]
                o = ost[cnt['rz'] % 2]
                cnt['rz'] += 1
                S.op('dve', lambda e: e.reciprocal(out=r.ap[64:128, :W], in_=Oc.ap[64:128, :W]), reads=[Oc.tok], writes=[r.tok])
                if kind == 'dense':
                    S.op('dve', lambda e: e.tensor_tensor(out=o.ap[0:64, :W], in0=Oc.ap[0:64, :W], in1=r.ap[64:128, :W], op=ALU.mult),
                         reads=[Oc.tok, r.tok], writes=[o.tok])
                    for g in range(G):
                        S.dma('sp', OT[u["orow"][g]:u["orow"][g] + 64, q0:q0 + nq], o.ap[0:64, g * nq:(g + 1) * nq], o.tok,
                              reads=[o.tok], pwrites=[tok_OT])
                else:
                    S.op('dve', lambda e: e.tensor_tensor(out=o1.ap[0:64, :nq], in0=Oc.ap[0:64, 0:nq], in1=r.ap[64:128, 0:nq], op=ALU.mult),
                         reads=[Oc.tok, r.tok], writes=[o1.tok])
                    S.op('dve', lambda e: e.tensor_tensor(out=o2.ap[0:64, :nq], in0=Oc.ap[0:64, nq:2 * nq], in1=r.ap[64:128, nq:2 * nq],
                                                          op=ALU.mult), reads=[Oc.tok, r.tok], writes=[o2.tok])
                    S.op('dve', lambda e: e.scalar_tensor_tensor(out=o1.ap[0:64, :nq], in0=o2.ap[0:64, :nq], scalar=lamc.ap[0:64, 0:1],
                                                                 in1=o1.ap[0:64, :nq], op0=ALU.mult, op1=ALU.add),
                         reads=[o1.tok, o2.tok, lamc.tok], writes=[o1.tok])
                    S.op('dve', lambda e: e.tensor_tensor(out=osq.ap[0:64, :nq], in0=o1.ap[0:64, :nq], in1=o1.ap[0:64, :nq], op=ALU.mult),
                         reads=[o1.tok], writes=[osq.tok])
                    return lambda: norm2(it, Oc, o)
                return None

            def norm2(it, Oc, o):
                    q0, nq, pair, first, last = it
                    mm_group(Oc.ap[0:64, :nq], [(ones_bf.ap[0:64, 0:64], osq.ap[0:64, :nq])], [ones_bf.tok, osq.tok], Oc.tok)
                    S.op('act', lambda e: e.activation(out=rsd.ap[0:64, :nq], in_=Oc.ap[0:64, :nq], func=AF.Ln,
                                                       bias=epsc.ap[0:64, :], scale=1.0 / 64),
                         reads=[Oc.tok, epsc.tok], writes=[rsd.tok])
                    S.op('act', lambda e: e.activation(out=rsd.ap[0:64, :nq], in_=rsd.ap[0:64, :nq], func=AF.Exp, scale=-0.5),
                         reads=[rsd.tok], writes=[rsd.tok])
                    S.op('dve', lambda e: e.scalar_tensor_tensor(out=o.ap[0:64, :nq], in0=o1.ap[0:64, :nq], scalar=subg.ap[0:64, 0:1],
                                                                 in1=rsd.ap[0:64, :nq], op0=ALU.mult, op1=ALU.mult),
                         reads=[o1.tok, subg.tok, rsd.tok], writes=[o.tok])
                    S.dma('sp', OT[u["orow"][0]:u["orow"][0] + 64, q0:q0 + nq], o.ap[0:64, :nq], o.tok,
                          reads=[o.tok], pwrites=[tok_OT])

            stq = [qk(items[j]) for j in range(min(LOOK, len(items)))]
            Oc = None
            pending = []
            for j, it in enumerate(items):
                if j + LOOK < len(items):
                    stq.append(qk(items[j + LOOK]))
                PTt = ex(stq[j], it)
                if it[3]:
                    Oc = Ob[cnt['o'] % 2]
                    cnt['o'] += 1
                pv(PTt, it, Oc)
                for pj, pf in list(pending):
                    if pj <= j:
                        pf()
                        pending.remove((pj, pf))
                if it[4]:
                    for pj, pf in pending:
                        pf()
                    pending = []
                    f2 = normalise(it, Oc)
                    if f2 is not None:
                        pending.append((j + 6, f2))
            for pj, pf in pending:
                pf()

        def win_unit(u, st, blocks):
            K, Q, V = st["K"], st["Q"], st["V"]
            items = []
            for b in blocks:
                if b < 64:
                    kbs = [(kb, mk) for kb, mk in ((b - 1, 0), (b, None), (b + 1, 1)) if 0 <= kb < 64] + [(64, None), (65, None)]
                else:
                    kbs = [(64, None), (65, None)]
                for g in range(2):
                    items.append((b, g, kbs))
            LOOK = 2

            def qk(it):
                b, g, kbs = it
                q0 = b * 128
                STt = STb[cnt['st'] % 2]
                cnt['st'] += 1
                for i, (kb, mk) in enumerate(kbs):
                    rows = slice(0, 64)
                    S.op('pe', lambda e, i=i, kb=kb, mk=mk, rows=rows: e.matmul(
                        STt.ap[:, i * 128:(i + 1) * 128], lhsT=K.ap[rows, kb * 128:(kb + 1) * 128], rhs=Q.ap[rows, g, q0:q0 + 128],
                        start=True, stop=(mk is None)),
                        reads=[K.tok, Q.tok], writes=[STt.tok] if i == 0 else [], pwrites=[] if i == 0 else [STt.tok],
                        inc=(i == len(kbs) - 1 and mk is None))
                    if mk is not None:
                        S.op('pe', lambda e, i=i, mk=mk: e.matmul(
                            STt.ap[:, i * 128:(i + 1) * 128], lhsT=masks.ap[:, 2, :], rhs=masks.ap[:, mk, :],
                            start=False, stop=True),
                            reads=[masks.tok], pwrites=[STt.tok], inc=(i == len(kbs) - 1))
                return STt

            def rest(STt, it):
                b, g, kbs = it
                q0 = b * 128
                n = len(kbs)
                PTt = PT[cnt['pt'] % 3]
                cnt['pt'] += 1
                S.op('act', lambda e: e.activation(out=PTt.ap[:, :128 * n], in_=STt.ap[:, :128 * n], func=AF.Exp, scale=u["scale"]),
                     reads=[STt.tok], writes=[PTt.tok])
                Oc = Ob[cnt['o'] % 2]
                cnt['o'] += 1
                for i, (kb, mk) in enumerate(kbs):
                    S.op('pe', lambda e, kb=kb, i=i: e.matmul(
                        Oc.ap[:, :128], lhsT=V.ap[:, kb, :], rhs=PTt.ap[:, i * 128:(i + 1) * 128], start=(i == 0), stop=False),
                        reads=[V.tok, PTt.tok], writes=[Oc.tok] if i == 0 else [], pwrites=[] if i == 0 else [Oc.tok],
                        inc=False)
                hd = u["heads"][g]
                S.op('pe', lambda e: e.matmul(Oc.ap[:, :128], lhsT=sinkrow.ap[0:1, hd, :], rhs=ones_bf.ap[0:1, 0:128],
                                              start=False, stop=True),
                     reads=[sinkrow.tok, ones_bf.tok], pwrites=[Oc.tok], inc=True)
                r = rz[cnt['rz'] % 2]
                o = ost[cnt['rz'] % 2]
                cnt['rz'] += 1
                S.op('dve', lambda e: e.reciprocal(out=r.ap[64:128, :128], in_=Oc.ap[64:128, :128]), reads=[Oc.tok], writes=[r.tok])
                S.op('dve', lambda e: e.tensor_tensor(out=o.ap[0:64, :128], in0=Oc.ap[0:64, :128], in1=r.ap[64:128, :128], op=ALU.mult),
                     reads=[Oc.tok, r.tok], writes=[o.tok])
                S.dma('sp', OT[u["orow"][g]:u["orow"][g] + 64, q0:q0 + 128], o.ap[0:64, :128], o.tok,
                      reads=[o.tok], pwrites=[tok_OT])

            LOOK = 1
            stq = [qk(items[j]) for j in range(min(LOOK, len(items)))]
            for j, it in enumerate(items):
                if j + LOOK < len(items):
                    stq.append(qk(items[j + LOOK]))
                rest(stq[j], it)

        if dbg and "attn_units" in dbg:
            units = [units[i_] for i_ in dbg["attn_units"]]
        nqlim = dbg.get("attn_nq", 10 ** 9) if dbg else 10 ** 9
        load_unit(units[0], sets[0])
        for ui, u in enumerate(units):
            st = sets[ui % 2]
            if ui + 1 < len(units):
                load_unit(units[ui + 1], sets[(ui + 1) % 2])
            allk = list(range(NKB))
            if u["kind"] == 'win':
                win_unit(u, st, [b for b in range(64 + (2 if need_ctx else 0)) if (b < nqlim or b >= 64)])
            else:
                nq = 256 if (u["G"] == 2 or u["kind"] == 'diff') else 512
                qch = [(q0, nq, allk) for q0 in range(0, TL, nq) if q0 // nq < nqlim]
                if need_ctx and nqlim > 0:
                    qch.append((TL, 256, [64, 65]))
                if qch:
                    dense_unit(u, st, qch)

    def phase_merge(l, chunks):
        P.reset()
        S.barrier()
        Wgt = B(P.alloc([128, 8, 4096], BF16), "Wgt")
        Wbr = B(P.alloc([128, 8, D], BF16), "Wbr")
        Wo = B(P.alloc([128, 8, D], BF16), "Wo")
        load_w_cast(Wgt, wgate_d[l].rearrange("(c p) j -> p c j", p=128))
        load_w_cast(Wbr, wbr_d[l].rearrange("i (c p) j -> p (i c) j", p=128))
        load_w_cast(Wo, wout_d[l].rearrange("(c p) j -> p c j", p=128))
        xb = [B(P.alloc([128, 8, 512], F32), f"xb{k}") for k in range(2)]
        otb = [B(P.alloc([128, 8, 512], BF16), f"ot{k}") for k in range(2)]
        ubs = [B(P.alloc([128, 8, 512], BF16), f"ub{k}") for k in range(2)]
        rs2 = B(P.alloc([128, 512], F32), "rs2")
        yacc = B(P.alloc([128, 8, 512], F32), "yacc")
        ybf = B(P.alloc([128, 8, 512], BF16), "ybf")
        rs = B(P.alloc([128, 512], F32), "rs")
        tmps = [B(P.alloc([128, 512], F32), f"tmp{k}") for k in range(2)]
        sig = [B(P.alloc([128, 512], F32), f"sig{k}") for k in range(2)]
        ssb = bank[0]
        ssb2 = bank[7]
        glb = [bank[1], bank[2]]
        zb = [bank[3], bank[4]]
        y2b = [bank[5], bank[6]]
        oview = OT.rearrange("(c p) t -> p c t", p=128)
        it = [0]

        def loads(n):
            ci = chunks[n]
            t0, N, w = chunk(ci)
            load_x(xb[n % 2], ci)
            S.dma('sp', otb[n % 2].ap[:, :, :N], oview[:, :, t0:t0 + N], otb[n % 2].tok, reads=[tok_OT], writes=[otb[n % 2].tok])
        loads(0)
        for n, ci in enumerate(chunks):
            t0, N, w = chunk(ci)
            k = n % 2
            if n + 1 < len(chunks):
                loads(n + 1)
            ub = ubs[k]
            if n == 0:
                adaln(xb[k], ub, N, w, 1, ssb2, rs, tmps)
            for d in range(8):
                if n + 1 < len(chunks) and d >= 3:
                    t0n, Nn, wn = chunk(chunks[n + 1])
                    k1 = (n + 1) % 2
                    if d == 3:
                        adaln_sq(xb[k1], ubs[k1], Nn)
                    elif d == 4:
                        adaln_ss(ubs[k1], Nn, ssb2)
                    elif d == 5:
                        adaln_fin_start(Nn, ssb2, rs)
                    else:
                        for c_ in range((d - 6) * 4, (d - 6) * 4 + 4):
                            adaln_fin_c(xb[k1], ubs[k1], Nn, wn, 1, rs, tmps, c_)
                for i in range(4):
                    gl = glb[it[0] % 2]
                    z = zb[it[0] % 2]
                    sg = sig[it[0] % 2]
                    it[0] += 1
                    mm_group(gl.ap[:, :N], [(Wgt.ap[:, kc, i * 1024 + d * 128:i * 1024 + (d + 1) * 128], ub.ap[:, kc, :N])
                                            for kc in range(8)], [Wgt.tok, ub.tok], gl.tok)
                    mm_group(z.ap[:, :N], [(Wbr.ap[:, i * 2 + kc, d * 128:(d + 1) * 128], otb[k].ap[:, i * 2 + kc, :N])
                                           for kc in range(2)], [Wbr.tok, otb[k].tok], z.tok)
                    S.op('act', lambda e, gl=gl, sg=sg: e.activation(out=sg.ap[:, :N], in_=gl.ap[:, :N], func=AF.Sigmoid),
                         reads=[gl.tok], writes=[sg.tok])
                    if i == 0:
                        S.op('dve', lambda e, d=d, z=z, sg=sg: e.tensor_tensor(out=yacc.ap[:, d, :N], in0=z.ap[:, :N], in1=sg.ap[:, :N],
                                                                               op=ALU.mult),
                             reads=[z.tok, sg.tok], writes=[yacc.tok] if d == 0 else [], pwrites=[] if d == 0 else [yacc.tok])
                    else:
                        S.op('dve', lambda e, z=z, sg=sg: e.tensor_tensor(out=sg.ap[:, :N], in0=z.ap[:, :N], in1=sg.ap[:, :N],
                                                                         op=ALU.mult), reads=[z.tok, sg.tok], writes=[sg.tok])
                        S.op(EW2, lambda e, d=d, sg=sg: e.tensor_tensor(out=yacc.ap[:, d, :N], in0=yacc.ap[:, d, :N],
                                                                          in1=sg.ap[:, :N], op=ALU.add),
                             reads=[sg.tok, yacc.tok], writes=[yacc.tok])
                S.op('act', lambda e, d=d: e.activation(out=ybf.ap[:, d, :N], in_=yacc.ap[:, d, :N], func=AF.Copy),
                     reads=[yacc.tok], writes=[ybf.tok] if d == 0 else [], pwrites=[] if d == 0 else [ybf.tok])
            for d2 in range(8):
                y2 = y2b[d2 % 2]
                mm_group(y2.ap[:, :N], [(Wo.ap[:, dd, d2 * 128:(d2 + 1) * 128], ybf.ap[:, dd, :N]) for dd in range(8)],
                         [Wo.tok, ybf.tok], y2.tok)
                S.op('act', lambda e, d2=d2, y2=y2: e.activation(out=ub.ap[:, d2, :N], in_=y2.ap[:, :N], func=AF.Square),
                     reads=[y2.tok], writes=[ub.tok] if d2 == 0 else [], pwrites=[] if d2 == 0 else [ub.tok])
                S.op('dve', lambda e, d2=d2, y2=y2: e.tensor_copy(out=yacc.ap[:, d2, :N], in_=y2.ap[:, :N]),
                     reads=[y2.tok, ub.tok], writes=[yacc.tok] if d2 == 0 else [], pwrites=[] if d2 == 0 else [yacc.tok])
            mm_group(ssb.ap[:, :N], [(ones_bf.ap, ub.ap[:, c, :N]) for c in range(8)], [ones_bf.tok, ub.tok], ssb.tok)
            rstd_from_ss(ssb, N, rs2, 1.0 / D)
            for d in range(8):
                tmp = tmps[d % 2]
                S.op('dve', lambda e, d=d, tmp=tmp: e.scalar_tensor_tensor(
                    out=tmp.ap[:, :N], in0=yacc.ap[:, d, :N], scalar=Gvec.ap[:, w, 8 + d:8 + d + 1],
                    in1=rs2.ap[:, :N], op0=ALU.mult, op1=ALU.mult),
                    reads=[yacc.tok, Gvec.tok, rs2.tok], writes=[tmp.tok])
                S.op(EW2, lambda e, d=d, tmp=tmp: e.tensor_tensor(out=xb[k].ap[:, d, :N], in0=xb[k].ap[:, d, :N],
                                                                    in1=tmp.ap[:, :N], op=ALU.add),
                     reads=[tmp.tok, xb[k].tok], writes=[xb[k].tok])
            S.dma('sp', xview[:, :, t0:t0 + N], xb[k].ap[:, :, :N], xb[k].tok, reads=[xb[k].tok], writes=[xtok[ci]])

    stop_after = dbg.get("stop_after") if dbg else None
    xin_view = xT_in.rearrange("(c p) t -> p c t", p=128)
    allc = list(range(NCH))
    latc = list(range(16))
    if dbg and "chunks" in dbg:
        allc = list(dbg["chunks"])
        latc = [c_ for c_ in allc if c_ < 16]
    done = False
    def plan():
        for l in range(DEPTH):
            need_ctx = l < DEPTH - 1
            yield ("mod", l), (lambda l=l: phase_mod(l))
            yield ("ffn1a", l), (lambda l=l: phase_ffn_a(l, 0, 0, allc, src_first=(xin_view if l == 0 else None)))
            yield ("ffn1", l), (lambda l=l: phase_ffn_b(l, 0, 0, allc, src_first=(xin_view if l == 0 else None)))
            yield ("proj", l), (lambda l=l: phase_proj(l, allc))
            yield ("attn", l), (lambda l=l, need_ctx=need_ctx: phase_attn(l, need_ctx))
            ch2 = allc if need_ctx else latc
            yield ("merge", l), (lambda l=l, ch2=ch2: phase_merge(l, ch2))
            yield ("ffn2a", l), (lambda l=l, ch2=ch2: phase_ffn_a(l, 1, 2, ch2))
            yield ("ffn2", l), (lambda l=l, ch2=ch2: phase_ffn_b(l, 1, 2, ch2, final_out=(l == DEPTH - 1)))
    for name, fn in plan():
        fn()
        if stop_after == name:
            break
    S.barrier()
    S.emit_all()
    return nc, S


def _perm(n_block, half):
    idx = np.arange(n_block)
    r = idx % half
    base = idx - r
    return base + (r + half // 2) % half


def _rope_tables(rot_dim):
    half = rot_dim // 2
    inv = 10000.0 ** (-np.arange(0, half, 2, dtype=np.float64) / half)
    t = np.arange(TL)
    row = (t // 64).astype(np.float64)
    col = (t % 64).astype(np.float64)
    nfr = half // 2
    C = np.ones((rot_dim, T), np.float64)
    Sg = np.zeros((rot_dim, T), np.float64)
    for i in range(rot_dim):
        r = i % half
        pos = row if i < half else col
        ang = pos * inv[r % nfr]
        C[i, :TL] = np.cos(ang)
        Sg[i, :TL] = np.sin(ang) * (-1.0 if r < nfr else 1.0)
    rep = 128 // rot_dim
    return np.tile(C, (rep, 1)).astype(np.float32), np.tile(Sg, (rep, 1)).astype(np.float32)


def _prep_common(inp):
    f = np.float32
    w_in = np.asarray(inp["w_in"], f)
    p64 = _perm(64, 32)
    p32 = _perm(32, 16)
    fm_cols = np.concatenate([np.arange(0, 384), np.arange(928, 1440), np.arange(1696, 2080),
                              np.arange(512, 896)])
    perm_fm = np.concatenate([np.concatenate([b0 + p64 for b0 in range(0, 384, 64)]),
                              np.concatenate([928 + b0 + p32 for b0 in range(0, 512, 32)]),
                              np.concatenate([1696 + b0 + p64 for b0 in range(0, 384, 64)])])
    kpe_cols = np.concatenate([np.arange(896, 928), 896 + p32])
    v_cols = np.concatenate([np.arange(384, 512), np.arange(1440, 1696), np.arange(2080, 2208)])
    uq = np.asarray(inp["mla_w_uq"], f)
    ukv = np.asarray(inp["mla_w_ukv"], f)
    n_cols = np.concatenate([np.arange(h * 96, h * 96 + 64) for h in range(4)])
    p_cols = np.concatenate([np.arange(h * 96 + 64, h * 96 + 96) for h in range(4)])
    pr_cols = np.concatenate([h * 96 + 64 + p32 for h in range(4)])
    kk_cols = np.concatenate([np.arange(h * 128, h * 128 + 64) for h in range(4)])
    kv_cols = np.concatenate([np.arange(h * 128 + 64, h * 128 + 128) for h in range(4)])
    b_mod = np.asarray(inp["b_mod"], f)
    g_pre = np.asarray(inp["g_pre"], f)
    g_post = np.asarray(inp["g_post"], f)
    gq = np.asarray(inp["gqa_q_norm"], f)
    gk = np.asarray(inp["gqa_k_norm"], f)
    mq = np.asarray(inp["mla_q_norm"], f)
    mkv = np.asarray(inp["mla_kv_norm"], f)
    sub = np.asarray(inp["diff_subln"], f)
    vecT = np.zeros((DEPTH, 128, NV), f)
    for l in range(DEPTH):
        vecT[l, :, 0:24] = g_pre[l].reshape(24, 128).T
        vecT[l, :, 24:48] = g_post[l].reshape(24, 128).T
        vecT[l, :, 48] = np.tile(gq[l], 2)
        vecT[l, :, 49] = np.tile(gq[l][p64], 2)
        vecT[l, :, 50] = np.tile(gk[l], 2)
        vecT[l, :, 51] = np.tile(gk[l][p64], 2)
        vecT[l, :, 52:54] = mq[l].reshape(2, 128).T
        vecT[l, :, 54] = mkv[l]
        vecT[l, :, 55] = np.tile(sub[l], 2)
    C64, S64 = _rope_tables(64)
    C32, S32 = _rope_tables(32)
    kj = np.arange(128)[:, None]
    qi = np.arange(128)[None, :]
    m_lo = np.where(kj >= qi, 0.0, -30000.0).astype(np.float32)
    m_hi = np.where(kj <= qi, 0.0, -30000.0).astype(np.float32)
    masks = np.stack([m_lo, m_hi, np.eye(128, dtype=np.float32)], axis=1).astype(ml_dtypes.bfloat16)
    c = np.ascontiguousarray
    return {
        "w_mod": c(np.asarray(inp["w_mod"], f)),
        "bmodT": c(b_mod.reshape(DEPTH, 72, 128).transpose(0, 2, 1)),
        "vecT": vecT,
        "sinkB": c(np.broadcast_to(np.asarray(inp["swa_sink"], f)[:, None, :], (DEPTH, 128, 4))),
        "lamB": c(np.broadcast_to(np.asarray(inp["diff_lambda"], f).reshape(DEPTH, 1, 128), (DEPTH, 128, 128))),
        "w_ffn_gate": c(np.asarray(inp["w_ffn_gate"], f)),
        "w_ffn_up": c(np.asarray(inp["w_ffn_up"], f)),
        "w_ffn_down": c(np.asarray(inp["w_ffn_down"], f)),
        "w_fm": c(w_in[:, :, fm_cols]),
        "w_fmr": c(w_in[:, :, perm_fm]),
        "w_kpe": c(w_in[:, :, kpe_cols]),
        "w_v": c(w_in[:, :, v_cols]),
        "w_gate": c(w_in[:, :, 2208:]),
        "wuq_n": c(uq[:, :, n_cols]),
        "wuq_p": c(uq[:, :, p_cols]),
        "wuq_pr": c(uq[:, :, pr_cols]),
        "wukv_k": c(ukv[:, :, kk_cols]),
        "wukv_v": c(ukv[:, :, kv_cols]),
        "w_branch": c(np.asarray(inp["w_branch"], f)),
        "w_out": c(np.asarray(inp["w_out"], f)),
        "ropeC64": C64, "ropeS64": S64, "ropeC32": C32, "ropeS32": S32,
        "masks": masks,
    }


def _prep_core(inp, b):
    f = np.float32
    xT = np.concatenate([np.asarray(inp["x"][b], f).T, np.asarray(inp["ctx"][b], f).T], axis=1)
    ccT = np.stack([np.asarray(inp["c"][b], f).reshape(8, 128).T, np.asarray(inp["c_ctx"], f).reshape(8, 128).T], axis=1)
    return {"xT": np.ascontiguousarray(xT), "ccT": np.ascontiguousarray(ccT)}


_CACHE = {}


def kernel(**inputs):
    if "nc" not in _CACHE:
        _CACHE["nc"] = build_program()[0]
    nc = _CACHE["nc"]
    common = _prep_common(inputs)
    in_maps = []
    for b in range(8):
        m = dict(common)
        m.update(_prep_core(inputs, b))
        in_maps.append(m)
    res = run_bass_kernel_spmd(nc, in_maps, core_ids=list(range(8)))
    out = np.stack([np.ascontiguousarray(np.asarray(r["outT"], np.float32).T) for r in res.results], axis=0)
    return out
```

```python
import math
import numpy as np
import ml_dtypes
import concourse.bass as bass
import concourse.mybir as mybir
from concourse.bass_utils import run_bass_kernel_spmd

F32 = mybir.dt.float32
BF16 = mybir.dt.bfloat16
U8 = mybir.dt.uint8
AF = mybir.ActivationFunctionType
ALU = mybir.AluOpType

D = 1024
TL = 8192
TC = 256
T = TL + TC
DFF = 2816
NF = DFF // 128
DEPTH = 2
EPS = 1e-6
NV = 56
ENGS = ("pe", "act", "dve", "pool", "sp")
EPOCH = 30000
EW2 = 'dve'


class Tok:
    __slots__ = ("name", "wr", "rd", "dsem", "dcnt", "base", "dq")

    def __init__(self, name=""):
        self.name = name
        self.wr = {}
        self.base = {}
        self.rd = {}
        self.dsem = None
        self.dcnt = 0
        self.dq = None


class _Rec:
    def __init__(self):
        self.call = None

    def __getattr__(self, name):
        def f(*a, **k):
            self.call = (name, a, k)
            return self
        return f


class Sched:
    def __init__(self, nc):
        self.nc = nc
        self.ops = {e: [] for e in ENGS}
        self.cnt = {e: 0 for e in ENGS}
        self.sems = {e: [] for e in ENGS}
        self.waited = {e: {} for e in ENGS}
        self.free_dsems = {'sw': [], 'hw': []}
        self.live_dtoks = []
        self.all_dsems = []
        self.n_ins = 0
        self.n_wait = 0

    def _esem(self, e, epoch):
        while len(self.sems[e]) <= epoch:
            self.sems[e].append(self.nc.alloc_semaphore(f"s_{e}_{len(self.sems[e])}"))
        return self.sems[e][epoch]

    def _dsem(self, tok, q):
        kind = 'sw' if q == 'pool' else 'hw'
        if tok.dsem is None:
            fl = self.free_dsems[kind]
            if fl:
                sem, cnt = fl.pop(0)
            else:
                sem = self.nc.alloc_semaphore(f"d{kind}_{len(self.all_dsems)}")
                self.all_dsems.append(sem)
                cnt = 0
            tok.dsem = sem
            tok.dcnt = cnt
            tok.dq = kind
            self.live_dtoks.append(tok)
        assert tok.dq == kind, f"token {tok.name} used by both SW and HW DMA queues"
        return tok.dsem

    @staticmethod
    def _ekey(e, seq):
        epoch, v = divmod(seq - 1, EPOCH)
        return ('e', e, epoch), v + 1

    def _need(self, e, key, ent, deps):
        if key[0] == 'e':
            val = ent
            sem = self._esem(key[1], key[2])
        else:
            tok = ent
            if tok.dsem is None:
                return
            val = tok.dcnt
            sem = tok.dsem
        if self.waited[e].get(key, 0) >= val:
            return
        if key not in deps or deps[key][1] < val:
            deps[key] = (sem, val)

    def _collect(self, e, reads, writes, pwrites):
        deps = {}
        skip = (e == 'pe')
        for t in reads:
            for dct in (t.base, t.wr):
                for key, ent in dct.items():
                    if key[0] == 'e' and key[1] == e and skip:
                        continue
                    self._need(e, key, ent, deps)
        for t in writes:
            for dct in (t.base, t.wr, t.rd):
                for key, ent in dct.items():
                    if key[0] == 'e' and key[1] == e and skip:
                        continue
                    self._need(e, key, ent, deps)
        for t in pwrites:
            for dct in (t.base, t.rd):
                for key, ent in dct.items():
                    if key[0] == 'e' and key[1] == e and skip:
                        continue
                    self._need(e, key, ent, deps)
        out = []
        for key, (sem, val) in deps.items():
            self.waited[e][key] = val
            out.append((sem, val))
        self.n_wait += len(out)
        return out

    def _record(self, key, ent, reads, writes, pwrites):
        for t in reads:
            if key[0] == 'e':
                if t.rd.get(key, 0) < ent:
                    t.rd[key] = ent
            else:
                t.rd[key] = ent
        for t in writes:
            t.base = {key: ent}
            t.wr = {}
            t.rd = {}
        for t in pwrites:
            if key[0] == 'e':
                if t.wr.get(key, 0) < ent:
                    t.wr[key] = ent
            else:
                t.wr[key] = ent

    def op(self, e, fn, reads=(), writes=(), pwrites=(), inc=True):
        waits = self._collect(e, reads, writes, pwrites)
        if inc:
            self.cnt[e] += 1
            seq = self.cnt[e]
        else:
            seq = self.cnt[e] + 1
        key, val = self._ekey(e, seq)
        sem = self._esem(e, key[2]) if inc else None
        self.n_ins += 1

        rec = _Rec()
        fn(rec)
        call = rec.call

        def emit(eng, waits=waits, call=call, sem=sem):
            for (s, v) in waits:
                eng.wait_ge(s, v)
            ins = getattr(eng, call[0])(*call[1], **call[2])
            if sem is not None:
                ins.then_inc(sem, 1)
        self.ops[e].append(emit)
        self._record(key, val, reads, writes, pwrites)

    def dma(self, q, out, in_, owner, reads=(), writes=(), pwrites=()):
        waits = self._collect(q, reads, writes, pwrites)
        sem = self._dsem(owner, q)
        owner.dcnt += 16
        self.n_ins += 1

        def emit(eng, waits=waits, sem=sem, out=out, in_=in_):
            for (s, v) in waits:
                eng.wait_ge(s, v)
            eng.dma_start(out=out, in_=in_).then_inc(sem, 16)
        self.ops[q].append(emit)
        self._record(('d', id(owner)), owner, reads, writes, pwrites)

    def barrier(self, engines=ENGS):
        targets = []
        for f in ENGS:
            if self.cnt[f] > 0:
                key, val = self._ekey(f, self.cnt[f])
                targets.append((key, self._esem(f, key[2]), val))
        dts = [(('d', id(t)), t.dsem, t.dcnt) for t in self.live_dtoks]
        for e in engines:
            waits = []
            for key, sem, val in targets:
                if key[1] == e:
                    continue
                if self.waited[e].get(key, 0) < val:
                    self.waited[e][key] = val
                    waits.append((sem, val))
            for key, sem, val in dts:
                if self.waited[e].get(key, 0) < val:
                    waits.append((sem, val))

            def emit(eng, waits=waits):
                for (s, v) in waits:
                    eng.wait_ge(s, v)
            self.ops[e].append(emit)
            self.n_wait += len(waits)
        if tuple(engines) == ENGS:
            for t in self.live_dtoks:
                self.free_dsems[t.dq].append((t.dsem, t.dcnt))
                t.dsem = None
                t.wr = {}
                t.rd = {}
                t.base = {}
            self.live_dtoks = []
            for e in ENGS:
                self.waited[e] = {k: v for k, v in self.waited[e].items() if k[0] == 'e'}

    def emit_all(self):
        nc = self.nc
        ops = self.ops
        with nc.Block() as block:
            @block.tensor
            def _(eng):
                for f in ops['pe']:
                    f(eng)

            @block.scalar
            def _(eng):
                for f in ops['act']:
                    f(eng)

            @block.vector
            def _(eng):
                for f in ops['dve']:
                    f(eng)

            @block.gpsimd
            def _(eng):
                for f in ops['pool']:
                    f(eng)

            @block.sync
            def _(eng):
                for f in ops['sp']:
                    f(eng)


class Pool:
    CAP = 212000

    def __init__(self, nc):
        self.t = nc.alloc_sbuf_tensor("sbpool", [128, self.CAP], U8)
        self.off = 0
        self.base = 0

    def set_base(self):
        self.base = self.off

    def reset(self):
        self.off = self.base

    def alloc(self, shape, dtype):
        esz = 4 if dtype == F32 else 2
        n = 1
        for s in shape[1:]:
            n *= s
        nbytes = (n * esz + 63) // 64 * 64
        assert self.off + nbytes <= self.CAP, f"SBUF overflow {self.off}+{nbytes}"
        v = self.t[:, self.off:self.off + n * esz].bitcast(dtype)
        self.off += nbytes
        if len(shape) == 3:
            v = v.rearrange("p (a b) -> p a b", b=shape[2])
        elif len(shape) == 4:
            v = v.rearrange("p (a b c) -> p a b c", b=shape[2], c=shape[3])
        return v


class B:
    def __init__(self, ap, name=""):
        self.ap = ap
        self.tok = Tok(name)


def build_program(dbg=None):
    nc = bass.Bass("TRN2", target_bir_lowering=False)
    S = Sched(nc)
    P = Pool(nc)

    def din(name, shape, dt=F32):
        return nc.dram_tensor(name, list(shape), dt, kind="ExternalInput").ap()

    def dscr(name, shape, dt):
        kind = "ExternalOutput" if (dbg and name in dbg) else "Internal"
        return nc.dram_tensor(name, list(shape), dt, kind=kind).ap()

    xT_in = din("xT", [D, T])
    ccT_d = din("ccT", [128, 2, 8])
    w_mod_d = din("w_mod", [DEPTH, D, 9 * D])
    bmodT_d = din("bmodT", [DEPTH, 128, 72])
    vecT_d = din("vecT", [DEPTH, 128, NV])
    sinkB_d = din("sinkB", [DEPTH, 128, 4])
    lamB_d = din("lamB", [DEPTH, 128, 128])
    wg_d = din("w_ffn_gate", [DEPTH, 2, D, DFF])
    wu_d = din("w_ffn_up", [DEPTH, 2, D, DFF])
    wd_d = din("w_ffn_down", [DEPTH, 2, DFF, D])
    wfm_d = din("w_fm", [DEPTH, D, 1664])
    wfmr_d = din("w_fmr", [DEPTH, D, 1280])
    wkpe_d = din("w_kpe", [DEPTH, D, 64])
    wv_d = din("w_v", [DEPTH, D, 512])
    wgate_d = din("w_gate", [DEPTH, D, 4096])
    wuqn_d = din("wuq_n", [DEPTH, 256, 256])
    wuqp_d = din("wuq_p", [DEPTH, 256, 128])
    wuqpr_d = din("wuq_pr", [DEPTH, 256, 128])
    wukvk_d = din("wukv_k", [DEPTH, 128, 256])
    wukvv_d = din("wukv_v", [DEPTH, 128, 256])
    wbr_d = din("w_branch", [DEPTH, 4, 256, D])
    wout_d = din("w_out", [DEPTH, D, D])
    rope_d = {k: din(k, [128, T]) for k in ("ropeC64", "ropeS64", "ropeC32", "ropeS32")}
    masks_d = din("masks", [128, 3, 128], BF16)
    outT_d = nc.dram_tensor("outT", [D, TL], F32, kind="ExternalOutput").ap()

    xT = dscr("xT_s", [D, T], F32)
    Hs = dscr("H_s", [DFF, T], BF16)
    QA = dscr("QA", [256, T], BF16); KA = dscr("KA", [128, T], BF16)
    QC = dscr("QC", [256, T], BF16); KC = dscr("KC", [256, T], BF16)
    QD = dscr("QD", [256, T], BF16); KD = dscr("KD", [128, T], BF16)
    QBn = dscr("QBn", [256, T], BF16); QBp = dscr("QBp", [128, T], BF16)
    KBn = dscr("KBn", [256, T], BF16); KBp = dscr("KBp", [32, T], BF16)
    VACD = dscr("VACD", [T, 512], BF16); VB = dscr("VB", [T, 256], BF16)
    OT = dscr("OT", [4 * 256, T], BF16)
    tok_attn_in = Tok("attn_in")
    tok_OT = Tok("OT")
    NCH = 17
    xtok = [Tok(f"x{c}") for c in range(NCH)]
    htok = [Tok(f"h{c}") for c in range(NCH)]

    def chunk(ci):
        return (ci * 512, 512 if ci < 16 else 256, 0 if ci < 16 else 1)

    psum = nc.alloc_psum_tensor("psum", [128, 4096], F32)
    bank = [B(psum[:, 512 * i:512 * (i + 1)], f"bank{i}") for i in range(8)]

    ones_bf = B(P.alloc([128, 128], BF16), "ones")
    bd_bf = B(P.alloc([128, 128], BF16), "bd")
    epsc = B(P.alloc([128, 1], F32), "eps")
    masks = B(P.alloc([128, 3, 128], BF16), "masks")
    onesf = B(P.alloc([128, 64], F32), "onesf")
    sinkrow = B(P.alloc([128, 4, 128], BF16), "sinkrow")
    cc = B(P.alloc([128, 2, 8], F32), "cc")
    modT = B(P.alloc([128, 2, 72], F32), "modT")
    vecT = B(P.alloc([128, NV], F32), "vecT")
    Avec = B(P.alloc([128, 2, 24], F32), "Avec")
    Gvec = B(P.alloc([128, 2, 24], F32), "Gvec")
    esink = B(P.alloc([128, 4], F32), "esink")
    lamc = B(P.alloc([128, 4], F32), "lamc")
    subg = B(P.alloc([128, 1], F32), "subg")
    P.set_base()

    S.op('dve', lambda e: e.memset(ones_bf.ap, 1.0), writes=[ones_bf.tok])
    S.op('dve', lambda e: e.memset(bd_bf.ap, 0.0), writes=[bd_bf.tok])
    S.op('dve', lambda e: e.memset(bd_bf.ap[0:64, 0:64], 1.0), pwrites=[bd_bf.tok])
    S.op('dve', lambda e: e.memset(bd_bf.ap[64:128, 64:128], 1.0), pwrites=[bd_bf.tok])
    S.op('dve', lambda e: e.memset(epsc.ap, EPS), writes=[epsc.tok])
    S.op('dve', lambda e: e.memset(onesf.ap, 1.0), writes=[onesf.tok])
    S.op('dve', lambda e: e.memset(sinkrow.ap, 0.0), writes=[sinkrow.tok])
    S.dma('pool', masks.ap, masks_d, masks.tok, writes=[masks.tok])
    S.dma('sp', cc.ap, ccT_d, cc.tok, writes=[cc.tok])
    S.op('act', lambda e: e.activation(out=cc.ap, in_=cc.ap, func=AF.Silu), reads=[cc.tok], writes=[cc.tok])

    def mm_group(out_ap, pairs, reads, out_tok):
        n = len(pairs)
        for i, (l, r) in enumerate(pairs):
            S.op('pe', (lambda e, l=l, r=r, i=i: e.matmul(out_ap, lhsT=l, rhs=r, start=(i == 0), stop=(i == n - 1))),
                 reads=reads, writes=[out_tok], inc=(i == n - 1))

    def rstd_from_ss(ss_bank, N, dst, inv_n, rows=slice(0, 128)):
        S.op('act', lambda e: e.activation(out=dst.ap[rows, :N], in_=ss_bank.ap[rows, :N], func=AF.Ln,
                                           bias=epsc.ap[rows, :], scale=inv_n),
             reads=[ss_bank.tok, epsc.tok], writes=[dst.tok])
        S.op('act', lambda e: e.activation(out=dst.ap[rows, :N], in_=dst.ap[rows, :N], func=AF.Exp, scale=-0.5),
             reads=[dst.tok], writes=[dst.tok])

    def load_w_cast(dst, src, q='pool'):
        if len(dst.ap.shape) == 3:
            for c_ in range(dst.ap.shape[1]):
                S.dma(q, dst.ap[:, c_, :], src[:, c_, :], dst.tok, pwrites=[dst.tok])
        else:
            S.dma(q, dst.ap, src, dst.tok, pwrites=[dst.tok])

    def adaln_sq(xb, ub, N):
        S.op('act', lambda e: e.activation(out=ub.ap[:, :, :N], in_=xb.ap[:, :, :N], func=AF.Square),
             reads=[xb.tok], writes=[ub.tok])

    def adaln_ss(ub, N, ssb):
        mm_group(ssb.ap[:, :N], [(ones_bf.ap, ub.ap[:, c, :N]) for c in range(8)], [ones_bf.tok, ub.tok], ssb.tok)

    def adaln_fin_start(N, ssb, rs):
        rstd_from_ss(ssb, N, rs, 1.0 / D)

    def adaln_fin_c(xb, ub, N, w, s, rs, tmps, c):
        tmp = tmps[c % len(tmps)]
        S.op('dve', lambda e: e.scalar_tensor_tensor(
            out=tmp.ap[:, :N], in0=xb.ap[:, c, :N], scalar=Avec.ap[:, w, s * 8 + c:s * 8 + c + 1],
            in1=rs.ap[:, :N], op0=ALU.mult, op1=ALU.mult),
            reads=[xb.tok, Avec.tok, rs.tok], writes=[tmp.tok])
        S.op('dve', lambda e: e.tensor_scalar(
            out=ub.ap[:, c, :N], in0=tmp.ap[:, :N], scalar1=modT.ap[:, w, s * 24 + c:s * 24 + c + 1], scalar2=None,
            op0=ALU.add),
            reads=[tmp.tok, modT.tok], writes=[ub.tok] if c == 0 else [], pwrites=[] if c == 0 else [ub.tok])

    def adaln_fin(xb, ub, N, w, s, ssb, rs, tmps):
        adaln_fin_start(N, ssb, rs)
        for c in range(8):
            adaln_fin_c(xb, ub, N, w, s, rs, tmps, c)

    def adaln(xb, ub, N, w, s, ssb, rs, tmps):
        adaln_sq(xb, ub, N)
        adaln_ss(ub, N, ssb)
        adaln_fin(xb, ub, N, w, s, ssb, rs, tmps)

    xview = xT.rearrange("(c p) t -> p c t", p=128)

    def load_x(xb, ci, src=None):
        t0, N, w = chunk(ci)
        v = (src if src is not None else xview)
        S.dma('sp', xb.ap[:, :, :N], v[:, :, t0:t0 + N], xb.tok, reads=[xtok[ci]], writes=[xb.tok])

    def phase_mod(l):
        P.reset()
        S.barrier()
        wp = [B(P.alloc([128, 8, 512], F32), f"wmod{i}") for i in range(4)]
        bm = B(P.alloc([128, 72], F32), "bm")
        lam_t = B(P.alloc([128, 128], F32), "lam_t")
        snk = B(P.alloc([128, 4], F32), "snk")
        mps = bank[0]
        S.dma('sp', bm.ap, bmodT_d[l], bm.tok, writes=[bm.tok])
        S.dma('sp', vecT.ap, vecT_d[l], vecT.tok, writes=[vecT.tok])
        S.dma('sp', lam_t.ap, lamB_d[l], lam_t.tok, writes=[lam_t.tok])
        S.dma('sp', snk.ap, sinkB_d[l], snk.tok, writes=[snk.tok])
        wv = w_mod_d[l].rearrange("(c p) j -> p c j", p=128)
        for jp in range(18):
            buf = wp[jp % 4]
            S.dma('sp', buf.ap, wv[:, :, jp * 512:(jp + 1) * 512], buf.tok, writes=[buf.tok])
            for jj in range(4):
                j = jp * 4 + jj
                mm_group(mps.ap[:, 2 * j:2 * j + 2],
                         [(buf.ap[:, kc, jj * 128:(jj + 1) * 128], cc.ap[:, :, kc]) for kc in range(8)],
                         [buf.tok, cc.tok], mps.tok)
        mv = mps.ap[:, 0:144].rearrange("p (j w) -> p j w", w=2)
        for w in range(2):
            S.op('dve', lambda e, w=w: e.tensor_tensor(out=modT.ap[:, w, :], in0=mv[:, :, w], in1=bm.ap, op=ALU.add),
                 reads=[mps.tok, bm.tok], writes=[modT.tok] if w == 0 else [], pwrites=[] if w == 0 else [modT.tok])
        for w in range(2):
            for s in range(3):
                S.op('dve', lambda e, w=w, s=s: e.scalar_tensor_tensor(
                    out=Avec.ap[:, w, s * 8:(s + 1) * 8], in0=modT.ap[:, w, s * 24 + 8:s * 24 + 16], scalar=1.0,
                    in1=vecT.ap[:, s * 8:(s + 1) * 8], op0=ALU.add, op1=ALU.mult),
                    reads=[modT.tok, vecT.tok], pwrites=[Avec.tok])
                S.op('dve', lambda e, w=w, s=s: e.scalar_tensor_tensor(
                    out=Gvec.ap[:, w, s * 8:(s + 1) * 8], in0=modT.ap[:, w, s * 24 + 16:s * 24 + 24],
                    scalar=(1.0 if s == 1 else 0.5),
                    in1=vecT.ap[:, 24 + s * 8:24 + (s + 1) * 8], op0=ALU.mult, op1=ALU.mult),
                    reads=[modT.tok, vecT.tok], pwrites=[Gvec.tok])
        S.op('act', lambda e: e.activation(out=esink.ap, in_=snk.ap, func=AF.Exp), reads=[snk.tok], writes=[esink.tok])
        for h_ in range(4):
            S.op('dve', lambda e, h_=h_: e.tensor_scalar(out=sinkrow.ap[0:1, h_, 64:128], in0=onesf.ap[0:1, 0:64],
                                                        scalar1=esink.ap[0:1, h_:h_ + 1], scalar2=None, op0=ALU.mult),
                 reads=[onesf.tok, esink.tok], writes=[sinkrow.tok] if h_ == 0 else [], pwrites=[] if h_ == 0 else [sinkrow.tok])
        lam_init = 0.8 - 0.6 * math.exp(-0.3 * l)
        pr = B(P.alloc([128, 64], F32), "pr")
        S.op('dve', lambda e: e.tensor_tensor(out=pr.ap[:, 0:32], in0=lam_t.ap[:, 0:32], in1=lam_t.ap[:, 32:64], op=ALU.mult),
             reads=[lam_t.tok], writes=[pr.tok])
        S.op('dve', lambda e: e.tensor_tensor(out=pr.ap[:, 32:64], in0=lam_t.ap[:, 64:96], in1=lam_t.ap[:, 96:128], op=ALU.mult),
             reads=[lam_t.tok], pwrites=[pr.tok])
        S.op('dve', lambda e: e.tensor_reduce(out=lamc.ap[:, 1:3], in_=pr.ap.rearrange("p (a b) -> p a b", b=32),
                                              axis=mybir.AxisListType.X, op=ALU.add),
             reads=[pr.tok], writes=[lamc.tok])
        S.op('act', lambda e: e.activation(out=lamc.ap[:, 1:3], in_=lamc.ap[:, 1:3], func=AF.Exp), reads=[lamc.tok], writes=[lamc.tok])
        S.op('dve', lambda e: e.scalar_tensor_tensor(out=lamc.ap[:, 0:1], in0=lamc.ap[:, 2:3], scalar=-lam_init,
                                                     in1=lamc.ap[:, 1:2], op0=ALU.add, op1=ALU.subtract),
             reads=[lamc.tok], writes=[lamc.tok])
        S.op('dve', lambda e: e.tensor_scalar(out=subg.ap, in0=vecT.ap[:, 55:56], scalar1=(1.0 - lam_init), scalar2=None,
                                              op0=ALU.mult), reads=[vecT.tok], writes=[subg.tok])

    def phase_ffn_a(l, i, s, chunks, src_first=None):
        P.reset()
        S.barrier()
        fblk = [(0, 2), (2, 6), (6, 14), (14, NF)]
        Wg_ap = P.alloc([128, 8, DFF], BF16)
        Wu_ap = P.alloc([128, 8, DFF], BF16)
        Wgb = [B(Wg_ap[:, :, a_ * 128:b_ * 128], f"Wg{a_}") for (a_, b_) in fblk]
        Wub = [B(Wu_ap[:, :, a_ * 128:b_ * 128], f"Wu{a_}") for (a_, b_) in fblk]
        fb_of = {}
        for bi_, (a_, b_) in enumerate(fblk):
            for f_ in range(a_, b_):
                fb_of[f_] = (bi_, (f_ - a_) * 128)
            for Wb_, src_ in ((Wgb[bi_], wg_d), (Wub[bi_], wu_d)):
                for kc in range(8):
                    S.dma('pool', Wb_.ap[:, kc, :], src_[l, i, kc * 128:(kc + 1) * 128, a_ * 128:b_ * 128], Wb_.tok,
                          pwrites=[Wb_.tok])
        xb = [B(P.alloc([128, 8, 512], F32), f"xb{k}") for k in range(2)]
        ub = [B(P.alloc([128, 8, 512], BF16), f"ub{k}") for k in range(2)]
        hst = [B(P.alloc([128, NF, 512], BF16), f"hst{k}") for k in range(2)]
        rs = [B(P.alloc([128, 512], F32), f"rs{k}") for k in range(2)]
        tmps = [B(P.alloc([128, 512], F32), f"tmp{k}") for k in range(2)]
        sg = [B(P.alloc([128, 512], F32), f"sg{k}") for k in range(2)]
        tmps2 = [B(P.alloc([128, 512], F32), f"tmpb{k}") for k in range(2)]
        ssb = bank[0]
        gb = [bank[1], bank[2]]
        upb = [bank[3], bank[4]]
        hview = Hs.rearrange("(f p) t -> p f t", p=128)
        load_x(xb[0], chunks[0], src_first)
        for n, ci in enumerate(chunks):
            t0, N, w = chunk(ci)
            k = n % 2
            if n + 1 < len(chunks):
                load_x(xb[(n + 1) % 2], chunks[n + 1], src_first)
            if n == 0:
                adaln(xb[k], ub[k], N, w, s, ssb, rs[k], tmps)
            for f in range(NF):
                if n + 1 < len(chunks) and (f in (4, 7, 11) or 12 <= f < 20):
                    t0n, Nn, wn = chunk(chunks[n + 1])
                    k1 = (n + 1) % 2
                    if f == 4:
                        adaln_sq(xb[k1], ub[k1], Nn)
                    elif f == 7:
                        adaln_ss(ub[k1], Nn, ssb)
                    elif f == 11:
                        adaln_fin_start(Nn, ssb, rs[k1])
                    else:
                        adaln_fin_c(xb[k1], ub[k1], Nn, wn, s, rs[k1], tmps2, f - 12)
                g = gb[f % 2]
                up = upb[f % 2]
                bi_, c0_ = fb_of[f]
                Wg, Wu = Wgb[bi_], Wub[bi_]
                mm_group(g.ap[:, :N], [(Wg.ap[:, kc, c0_:c0_ + 128], ub[k].ap[:, kc, :N]) for kc in range(8)],
                         [Wg.tok, ub[k].tok], g.tok)
                mm_group(up.ap[:, :N], [(Wu.ap[:, kc, c0_:c0_ + 128], ub[k].ap[:, kc, :N]) for kc in range(8)],
                         [Wu.tok, ub[k].tok], up.tok)
                sgt = sg[f % 2]
                S.op('act', lambda e, g=g, sgt=sgt: e.activation(out=sgt.ap[:, :N], in_=g.ap[:, :N], func=AF.Silu),
                     reads=[g.tok], writes=[sgt.tok])
                S.op('dve', lambda e, f=f, up=up, sgt=sgt: e.tensor_tensor(out=hst[k].ap[:, f, :N], in0=up.ap[:, :N],
                                                                         in1=sgt.ap[:, :N], op=ALU.mult),
                     reads=[up.tok, sgt.tok], writes=[hst[k].tok] if f == 0 else [], pwrites=[] if f == 0 else [hst[k].tok])
            S.dma('sp', hview[:, :, t0:t0 + N], hst[k].ap[:, :, :N], hst[k].tok, reads=[hst[k].tok], writes=[htok[ci]])

    def phase_ffn_b(l, i, s, chunks, src_first=None, final_out=False):
        P.reset()
        S.barrier()
        dblk = [(0, 2), (2, 4), (4, 8)]
        Wd_ap = P.alloc([128, NF, D], BF16)
        Wdb = [B(Wd_ap[:, :, a_ * 128:b_ * 128], f"Wd{a_}") for (a_, b_) in dblk]
        db_of = {}
        for bi_, (a_, b_) in enumerate(dblk):
            for d_ in range(a_, b_):
                db_of[d_] = (bi_, (d_ - a_) * 128)
            for f in range(NF):
                S.dma('pool', Wdb[bi_].ap[:, f, :], wd_d[l, i, f * 128:(f + 1) * 128, a_ * 128:b_ * 128], Wdb[bi_].tok,
                      pwrites=[Wdb[bi_].tok])
        xb = [B(P.alloc([128, 8, 512], F32), f"xb{k}") for k in range(2)]
        hb = [B(P.alloc([128, NF, 512], BF16), f"hb{k}") for k in range(2)]
        yb = B(P.alloc([128, 8, 512], F32), "yb")
        ysq = B(P.alloc([128, 8, 512], BF16), "ysq")
        rs = B(P.alloc([128, 512], F32), "rs")
        tmps = [B(P.alloc([128, 512], F32), f"tmp{k}") for k in range(2)]
        ypb = [bank[1], bank[2], bank[3]]
        ssb = bank[0]
        hview = Hs.rearrange("(f p) t -> p f t", p=128)
        oview = outT_d.rearrange("(c p) t -> p c t", p=128)

        def loads(n):
            ci = chunks[n]
            t0, N, w = chunk(ci)
            S.dma('sp', hb[n % 2].ap[:, :, :N], hview[:, :, t0:t0 + N], hb[n % 2].tok, reads=[htok[ci]], writes=[hb[n % 2].tok])
            load_x(xb[n % 2], ci, src_first)
        loads(0)
        for n, ci in enumerate(chunks):
            t0, N, w = chunk(ci)
            k = n % 2
            if n + 1 < len(chunks):
                loads(n + 1)
            lvl = dbg.get("lvl", 9) if dbg else 9
            for d in range(8 if lvl >= 2 else 0):
                yp = ypb[d % 3]
                bi_, c0_ = db_of[d]
                Wd = Wdb[bi_]
                mm_group(yp.ap[:, :N], [(Wd.ap[:, f, c0_:c0_ + 128], hb[k].ap[:, f, :N]) for f in range(NF)],
                         [Wd.tok, hb[k].tok], yp.tok)
                var = dbg.get("var", "ab") if dbg else "ab"
                if "a" in var:
                    S.op('act', lambda e, d=d, yp=yp: e.activation(out=ysq.ap[:, d, :N], in_=yp.ap[:, :N], func=AF.Square),
                         reads=[yp.tok], writes=[ysq.tok] if d == 0 else [], pwrites=[] if d == 0 else [ysq.tok])
                if "b" in var:
                    S.op('dve', lambda e, d=d, yp=yp: e.tensor_copy(out=yb.ap[:, d, :N], in_=yp.ap[:, :N]),
                         reads=[yp.tok, ysq.tok], writes=[yb.tok] if d == 0 else [], pwrites=[] if d == 0 else [yb.tok])
            if lvl >= 3:
                mm_group(ssb.ap[:, :N], [(ones_bf.ap, ysq.ap[:, c, :N]) for c in range(8)], [ones_bf.tok, ysq.tok], ssb.tok)
                rstd_from_ss(ssb, N, rs, 1.0 / D)
            for d in range(8 if lvl >= 4 else 0):
                tmp = tmps[d % 2]
                S.op('dve', lambda e, d=d, tmp=tmp: e.scalar_tensor_tensor(
                    out=tmp.ap[:, :N], in0=yb.ap[:, d, :N], scalar=Gvec.ap[:, w, s * 8 + d:s * 8 + d + 1],
                    in1=rs.ap[:, :N], op0=ALU.mult, op1=ALU.mult),
                    reads=[yb.tok, Gvec.tok, rs.tok], writes=[tmp.tok])
                S.op(EW2, lambda e, d=d, tmp=tmp: e.tensor_tensor(out=xb[k].ap[:, d, :N], in0=xb[k].ap[:, d, :N],
                                                                    in1=tmp.ap[:, :N], op=ALU.add),
                     reads=[tmp.tok, xb[k].tok], writes=[xb[k].tok])
            if final_out:
                S.dma('sp', oview[:, :, t0:t0 + N], xb[k].ap[:, :, :N], xb[k].tok, reads=[xb[k].tok], writes=[xtok[ci]])
            else:
                S.dma('sp', xview[:, :, t0:t0 + N], xb[k].ap[:, :, :N], xb[k].tok, reads=[xb[k].tok], writes=[xtok[ci]])

    def phase_proj(l, chunks):
        P.reset()
        S.barrier()
        Wfm = B(P.alloc([128, 8, 1664], BF16), "Wfm")
        Wfr = B(P.alloc([128, 8, 1280], BF16), "Wfr")
        Wkp = B(P.alloc([128, 8, 64], BF16), "Wkp")
        Wv = B(P.alloc([128, 8, 512], BF16), "Wv")
        Wqn = B(P.alloc([128, 2, 256], BF16), "Wqn")
        Wqp = B(P.alloc([128, 2, 128], BF16), "Wqp")
        Wqr = B(P.alloc([128, 2, 128], BF16), "Wqr")
        Wkk = B(P.alloc([128, 256], BF16), "Wkk")
        Wkv = B(P.alloc([128, 256], BF16), "Wkv")
        for Wt, src in ((Wfm, wfm_d), (Wfr, wfmr_d), (Wkp, wkpe_d), (Wv, wv_d)):
            load_w_cast(Wt, src[l].rearrange("(c p) j -> p c j", p=128))
        for Wt, src in ((Wqn, wuqn_d), (Wqp, wuqp_d), (Wqr, wuqpr_d)):
            load_w_cast(Wt, src[l].rearrange("(c p) j -> p c j", p=128))
        load_w_cast(Wkk, wukvk_d[l])
        load_w_cast(Wkv, wukvv_d[l])
        xb = [B(P.alloc([128, 8, 512], F32), f"xb{k}") for k in range(2)]
        ubs = [B(P.alloc([128, 8, 512], BF16), f"ub{k}") for k in range(2)]
        rss = [B(P.alloc([128, 512], F32), f"rs{k}") for k in range(2)]
        tmps = [B(P.alloc([128, 512], F32), f"tmp{k}") for k in range(2)]
        tabs = [{k: B(P.alloc([128, 512], F32), f"{k}{j}") for k in rope_d} for j in range(2)]
        NST = 6
        stg = [B(P.alloc([128, 512], BF16), f"stg{j}") for j in range(NST)]
        stg_i = [0]
        t1s = [B(P.alloc([128, 512], F32), f"t1_{j}") for j in range(2)]
        t2s = [B(P.alloc([128, 512], F32), f"t2_{j}") for j in range(2)]
        sqa = B(P.alloc([128, 512], BF16), "sqa")
        rsa = B(P.alloc([128, 512], F32), "rsa")
        cqn = B(P.alloc([128, 2, 512], BF16), "cqn")
        cqsq = B(P.alloc([128, 2, 512], BF16), "cqsq")
        ckvn = B(P.alloc([128, 512], BF16), "ckvn")
        vst = [B(P.alloc([128, 4, 512], BF16), f"vst{j}") for j in range(2)]
        vbst = [B(P.alloc([128, 4, 256], BF16), f"vbst{j}") for j in range(2)]
        ssb = bank[0]
        ssb2 = bank[7]
        p1b = [bank[1], bank[2]]
        p2b = [bank[3], bank[4]]
        pxb = [bank[5], bank[6]]
        gi = [0]

        def next_stg():
            b_ = stg[stg_i[0] % NST]
            stg_i[0] += 1
            return b_

        def store(dst_ap, st, rows, N):
            S.dma('sp', dst_ap, st.ap[rows, :N], st.tok, reads=[st.tok], pwrites=[tok_attn_in])

        def loads(n):
            ci = chunks[n]
            t0, N, w = chunk(ci)
            load_x(xb[n % 2], ci)
            for kk, src in rope_d.items():
                tb = tabs[n % 2][kk]
                S.dma('sp', tb.ap[:, :N], src[:, t0:t0 + N], tb.tok, writes=[tb.tok])
        loads(0)
        for n_, ci in enumerate(chunks):
            t0, N, w = chunk(ci)
            k = n_ % 2
            if n_ + 1 < len(chunks):
                loads(n_ + 1)
            tb = tabs[k]
            ub = ubs[k]
            if n_ == 0:
                adaln(xb[k], ub, N, w, 1, ssb2, rss[k], tmps)
            groups = [(0, 'A', QA, 0), (1, 'A', QA, 128), (2, 'A', KA, 0),
                      (3, 'C', QC, 0), (4, 'C', QC, 128), (5, 'C', KC, 0), (6, 'C', KC, 128),
                      (7, 'D', QD, 0), (8, 'D', QD, 128), (9, 'D', KD, 0)]
            for (g, kind, dst, r0) in groups:
                if n_ + 1 < len(chunks):
                    t0n, Nn, wn = chunk(chunks[n_ + 1])
                    k1 = (n_ + 1) % 2
                    if g == 1:
                        adaln_sq(xb[k1], ubs[k1], Nn)
                    elif g == 3:
                        adaln_ss(ubs[k1], Nn, ssb2)
                    elif g == 5:
                        adaln_fin_start(Nn, ssb2, rss[k1])
                    elif g >= 6:
                        for c_ in ((g - 6) * 2, (g - 6) * 2 + 1):
                            adaln_fin_c(xb[k1], ubs[k1], Nn, wn, 1, rss[k1], tmps, c_)
                p1 = p1b[gi[0] % 2]
                p2 = p2b[gi[0] % 2]
                t1 = t1s[gi[0] % 2]
                t2 = t2s[gi[0] % 2]
                gi[0] += 1
                mm_group(p1.ap[:, :N], [(Wfm.ap[:, kc, g * 128:(g + 1) * 128], ub.ap[:, kc, :N]) for kc in range(8)],
                         [Wfm.tok, ub.tok], p1.tok)
                mm_group(p2.ap[:, :N], [(Wfr.ap[:, kc, g * 128:(g + 1) * 128], ub.ap[:, kc, :N]) for kc in range(8)],
                         [Wfr.tok, ub.tok], p2.tok)
                st = next_stg()
                if kind == 'A':
                    Ct, St = tb["ropeC64"], tb["ropeS64"]
                    gcol = 48 if g < 2 else 50
                    S.op('act', lambda e, p1=p1: e.activation(out=sqa.ap[:, :N], in_=p1.ap[:, :N], func=AF.Square),
                         reads=[p1.tok], writes=[sqa.tok])
                    mm_group(ssb.ap[:, :N], [(bd_bf.ap, sqa.ap[:, :N])], [bd_bf.tok, sqa.tok], ssb.tok)
                    rstd_from_ss(ssb, N, rsa, 1.0 / 64)
                    S.op('dve', lambda e, p1=p1, t1=t1, Ct=Ct, gcol=gcol: e.scalar_tensor_tensor(
                        out=t1.ap[:, :N], in0=p1.ap[:, :N], scalar=vecT.ap[:, gcol:gcol + 1], in1=Ct.ap[:, :N],
                        op0=ALU.mult, op1=ALU.mult), reads=[p1.tok, vecT.tok, Ct.tok, sqa.tok], writes=[t1.tok])
                    S.op('dve', lambda e, p2=p2, t2=t2, St=St, gcol=gcol: e.scalar_tensor_tensor(
                        out=t2.ap[:, :N], in0=p2.ap[:, :N], scalar=vecT.ap[:, gcol + 1:gcol + 2], in1=St.ap[:, :N],
                        op0=ALU.mult, op1=ALU.mult), reads=[p2.tok, vecT.tok, St.tok], writes=[t2.tok])
                    S.op(EW2, lambda e, t1=t1, t2=t2: e.tensor_tensor(out=t1.ap[:, :N], in0=t1.ap[:, :N], in1=t2.ap[:, :N],
                                                                        op=ALU.add), reads=[t1.tok, t2.tok], writes=[t1.tok])
                    S.op(EW2, lambda e, t1=t1, st=st: e.tensor_tensor(out=st.ap[:, :N], in0=t1.ap[:, :N], in1=rsa.ap[:, :N],
                                                                        op=ALU.mult), reads=[t1.tok, rsa.tok], writes=[st.tok])
                else:
                    if kind == 'C':
                        Ct, St = tb["ropeC32"], tb["ropeS32"]
                    else:
                        Ct, St = tb["ropeC64"], tb["ropeS64"]
                    S.op('dve', lambda e, p1=p1, t1=t1, Ct=Ct: e.tensor_tensor(out=t1.ap[:, :N], in0=p1.ap[:, :N], in1=Ct.ap[:, :N],
                                                                              op=ALU.mult), reads=[p1.tok, Ct.tok], writes=[t1.tok])
                    S.op('dve', lambda e, p2=p2, t2=t2, St=St: e.tensor_tensor(out=t2.ap[:, :N], in0=p2.ap[:, :N], in1=St.ap[:, :N],
                                                                              op=ALU.mult), reads=[p2.tok, St.tok], writes=[t2.tok])
                    S.op(EW2, lambda e, t1=t1, t2=t2, st=st: e.tensor_tensor(out=st.ap[:, :N], in0=t1.ap[:, :N], in1=t2.ap[:, :N],
                                                                               op=ALU.add), reads=[t1.tok, t2.tok], writes=[st.tok])
                store(dst[r0:r0 + 128, t0:t0 + N], st, slice(0, 128), N)
            vs = vst[k]

            def v_tile(j):
                if j >= N // 128:
                    return
                px = pxb[j % 2]
                mm_group(px.ap[:, :512], [(ub.ap[:, kc, j * 128:(j + 1) * 128], Wv.ap[:, kc, :]) for kc in range(8)],
                         [Wv.tok, ub.tok], px.tok)
                S.op('act', lambda e: e.activation(out=vs.ap[:, j, :], in_=px.ap[:, :512], func=AF.Copy),
                     reads=[px.tok], writes=[vs.tok] if j == 0 else [], pwrites=[] if j == 0 else [vs.tok])
            pq = [p1b[0], p1b[1]]
            for c2 in range(2):
                mm_group(pq[c2].ap[:, :N], [(Wfm.ap[:, kc, (10 + c2) * 128:(11 + c2) * 128], ub.ap[:, kc, :N]) for kc in range(8)],
                         [Wfm.tok, ub.tok], pq[c2].tok)
                S.op('act', lambda e, c2=c2: e.activation(out=cqsq.ap[:, c2, :N], in_=pq[c2].ap[:, :N], func=AF.Square),
                     reads=[pq[c2].tok], writes=[cqsq.tok] if c2 == 0 else [], pwrites=[] if c2 == 0 else [cqsq.tok])
            v_tile(0)
            mm_group(ssb.ap[:, :N], [(ones_bf.ap, cqsq.ap[:, c2, :N]) for c2 in range(2)], [ones_bf.tok, cqsq.tok], ssb.tok)
            rstd_from_ss(ssb, N, rsa, 1.0 / 256)
            v_tile(1)
            for c2 in range(2):
                S.op('dve', lambda e, c2=c2: e.scalar_tensor_tensor(
                    out=cqn.ap[:, c2, :N], in0=pq[c2].ap[:, :N], scalar=vecT.ap[:, 52 + c2:53 + c2], in1=rsa.ap[:, :N],
                    op0=ALU.mult, op1=ALU.mult), reads=[pq[c2].tok, vecT.tok, rsa.tok],
                    writes=[cqn.tok] if c2 == 0 else [], pwrites=[] if c2 == 0 else [cqn.tok])
            for g2 in range(2):
                px = pxb[g2]
                mm_group(px.ap[:, :N], [(Wqn.ap[:, kc, g2 * 128:(g2 + 1) * 128], cqn.ap[:, kc, :N]) for kc in range(2)],
                         [Wqn.tok, cqn.tok], px.tok)
                st = next_stg()
                S.op('act', lambda e, px=px, st=st: e.activation(out=st.ap[:, :N], in_=px.ap[:, :N], func=AF.Copy),
                     reads=[px.tok], writes=[st.tok])
                store(QBn[g2 * 128:(g2 + 1) * 128, t0:t0 + N], st, slice(0, 128), N)
            p1, p2 = p2b[0], p2b[1]
            mm_group(p1.ap[:, :N], [(Wqp.ap[:, kc, :], cqn.ap[:, kc, :N]) for kc in range(2)], [Wqp.tok, cqn.tok], p1.tok)
            mm_group(p2.ap[:, :N], [(Wqr.ap[:, kc, :], cqn.ap[:, kc, :N]) for kc in range(2)], [Wqr.tok, cqn.tok], p2.tok)
            t1, t2 = t1s[0], t2s[0]
            st = next_stg()
            Ct, St = tb["ropeC32"], tb["ropeS32"]
            S.op('dve', lambda e: e.tensor_tensor(out=t1.ap[:, :N], in0=p1.ap[:, :N], in1=Ct.ap[:, :N], op=ALU.mult),
                 reads=[p1.tok, Ct.tok], writes=[t1.tok])
            S.op('dve', lambda e: e.tensor_tensor(out=t2.ap[:, :N], in0=p2.ap[:, :N], in1=St.ap[:, :N], op=ALU.mult),
                 reads=[p2.tok, St.tok], writes=[t2.tok])
            S.op(EW2, lambda e, st=st: e.tensor_tensor(out=st.ap[:, :N], in0=t1.ap[:, :N], in1=t2.ap[:, :N], op=ALU.add),
                 reads=[t1.tok, t2.tok], writes=[st.tok])
            store(QBp[:, t0:t0 + N], st, slice(0, 128), N)
            pk = p1b[0]
            mm_group(pk.ap[:, :N], [(Wfm.ap[:, kc, 12 * 128:13 * 128], ub.ap[:, kc, :N]) for kc in range(8)],
                     [Wfm.tok, ub.tok], pk.tok)
            S.op('act', lambda e: e.activation(out=sqa.ap[:, :N], in_=pk.ap[:, :N], func=AF.Square),
                 reads=[pk.tok], writes=[sqa.tok])
            v_tile(2)
            mm_group(ssb.ap[:, :N], [(ones_bf.ap, sqa.ap[:, :N])], [ones_bf.tok, sqa.tok], ssb.tok)
            rstd_from_ss(ssb, N, rsa, 1.0 / 128)
            v_tile(3)
            S.op('dve', lambda e: e.scalar_tensor_tensor(
                out=ckvn.ap[:, :N], in0=pk.ap[:, :N], scalar=vecT.ap[:, 54:55], in1=rsa.ap[:, :N],
                op0=ALU.mult, op1=ALU.mult), reads=[pk.tok, vecT.tok, rsa.tok], writes=[ckvn.tok])
            for g2 in range(2):
                px = pxb[g2]
                mm_group(px.ap[:, :N], [(Wkk.ap[:, g2 * 128:(g2 + 1) * 128], ckvn.ap[:, :N])], [Wkk.tok, ckvn.tok], px.tok)
                st = next_stg()
                S.op('act', lambda e, px=px, st=st: e.activation(out=st.ap[:, :N], in_=px.ap[:, :N], func=AF.Copy),
                     reads=[px.tok], writes=[st.tok])
                store(KBn[g2 * 128:(g2 + 1) * 128, t0:t0 + N], st, slice(0, 128), N)
            p1, p2 = p2b[0], p2b[1]
            mm_group(p1.ap[0:32, :N], [(Wkp.ap[:, kc, 0:32], ub.ap[:, kc, :N]) for kc in range(8)], [Wkp.tok, ub.tok], p1.tok)
            mm_group(p2.ap[0:32, :N], [(Wkp.ap[:, kc, 32:64], ub.ap[:, kc, :N]) for kc in range(8)], [Wkp.tok, ub.tok], p2.tok)
            t1, t2 = t1s[1], t2s[1]
            st = next_stg()
            S.op('dve', lambda e: e.tensor_tensor(out=t1.ap[0:32, :N], in0=p1.ap[0:32, :N], in1=Ct.ap[0:32, :N], op=ALU.mult),
                 reads=[p1.tok, Ct.tok], writes=[t1.tok])
            S.op('dve', lambda e: e.tensor_tensor(out=t2.ap[0:32, :N], in0=p2.ap[0:32, :N], in1=St.ap[0:32, :N], op=ALU.mult),
                 reads=[p2.tok, St.tok], writes=[t2.tok])
            S.op(EW2, lambda e, st=st: e.tensor_tensor(out=st.ap[0:32, :N], in0=t1.ap[0:32, :N], in1=t2.ap[0:32, :N], op=ALU.add),
                 reads=[t1.tok, t2.tok], writes=[st.tok])
            store(KBp[:, t0:t0 + N], st, slice(0, 32), N)
            vb = vbst[k]
            for j in range(N // 128):
                py = p1b[j % 2]
                mm_group(py.ap[:, :256], [(ckvn.ap[:, j * 128:(j + 1) * 128], Wkv.ap[:, :])], [Wkv.tok, ckvn.tok], py.tok)
                S.op('dve', lambda e, py=py, j=j: e.tensor_copy(out=vb.ap[:, j, :], in_=py.ap[:, :256]),
                     reads=[py.tok], writes=[vb.tok] if j == 0 else [], pwrites=[] if j == 0 else [vb.tok])
            nj = N // 128
            S.dma('sp', VACD[t0:t0 + N, :].rearrange("(j p) c -> p j c", p=128), vs.ap[:, :nj, :], vs.tok,
                  reads=[vs.tok], pwrites=[tok_attn_in])
            S.dma('sp', VB[t0:t0 + N, :].rearrange("(j p) c -> p j c", p=128), vb.ap[:, :nj, :], vb.tok,
                  reads=[vb.tok], pwrites=[tok_attn_in])

    def phase_attn(l, need_ctx):
        P.reset()
        S.barrier()
        NKB = T // 128
        sets = []
        for j in range(2):
            sets.append(dict(K=B(P.alloc([128, T], BF16), f"K{j}"), Q=B(P.alloc([128, 2, T], BF16), f"Q{j}"),
                             V=B(P.alloc([128, NKB, 128], BF16), f"V{j}")))
        for j in range(2):
            S.op(EW2, lambda e, j=j: e.memset(sets[j]["V"].ap[:, :, 64:128], 1.0), pwrites=[sets[j]["V"].tok])
        PT = [B(P.alloc([128, 1024], BF16), f"PT{j}") for j in range(3)]
        rz = [B(P.alloc([128, 512], F32), f"rz{j}") for j in range(2)]
        ost = [B(P.alloc([128, 512], BF16), f"ost{j}") for j in range(2)]
        o1 = B(P.alloc([128, 256], F32), "o1")
        o2 = B(P.alloc([128, 256], F32), "o2")
        osq = B(P.alloc([128, 256], BF16), "osq")
        rsd = B(P.alloc([128, 256], F32), "rsd")
        STb = [B(psum[:, 0:1024], "ST0"), B(psum[:, 1024:2048], "ST1"), B(psum[:, 2048:3072], "ST2")]
        Ob = [bank[6], bank[7]]
        cnt = dict(pt=0, st=0, o=0, rz=0)

        units = []
        for kv in range(2):
            units.append(dict(kind='dense', G=2, dk=64, scale=0.125,
                              kparts=[(KA[64 * kv:64 * kv + 64, :], 0, 64)],
                              qparts=[(QA[(2 * kv + g) * 64:(2 * kv + g + 1) * 64, :], g, 0, 64) for g in range(2)],
                              V=VACD[:, 64 * kv:64 * kv + 64], orow=[(2 * kv + g) * 64 for g in range(2)]))
        for h in range(4):
            units.append(dict(kind='dense', G=1, dk=96, scale=96 ** -0.5,
                              kparts=[(KBn[64 * h:64 * h + 64, :], 0, 64), (KBp[0:32, :], 64, 32)],
                              qparts=[(QBn[64 * h:64 * h + 64, :], 0, 0, 64), (QBp[32 * h:32 * h + 32, :], 0, 64, 32)],
                              V=VB[:, 64 * h:64 * h + 64], orow=[256 + 64 * h]))
        for h in range(4):
            units.append(dict(kind='diff', G=1, dk=64, scale=32 ** -0.5,
                              kparts=[(KC[64 * h:64 * h + 64, :], 0, 64)],
                              qparts=[(QC[64 * h:64 * h + 32, :], 0, 0, 32), (QC[64 * h + 32:64 * h + 64, :], 1, 32, 32)],
                              V=VACD[:, 128 + 64 * h:128 + 64 * h + 64], orow=[512 + 64 * h]))
        for kv in range(2):
            units.append(dict(kind='win', G=2, dk=64, scale=0.125,
                              kparts=[(KD[64 * kv:64 * kv + 64, :], 0, 64)],
                              qparts=[(QD[(2 * kv + g) * 64:(2 * kv + g + 1) * 64, :], g, 0, 64) for g in range(2)],
                              V=VACD[:, 384 + 64 * kv:384 + 64 * kv + 64], orow=[768 + (2 * kv + g) * 64 for g in range(2)],
                              heads=[2 * kv, 2 * kv + 1]))

        def load_unit(u, st):
            K, Q, V = st["K"], st["Q"], st["V"]
            dk_ = u["dk"]
            S.op('dve', lambda e: e.memset(K.ap[64:128, :], 0.0), writes=[K.tok])
            S.op('dve', lambda e: e.memset(Q.ap[64:128, :, :], 0.0), writes=[Q.tok])
            for (src, p0, n) in u["kparts"]:
                S.dma('sp', K.ap[p0:p0 + n, :], src, K.tok, reads=[tok_attn_in], pwrites=[K.tok])
            if u["kind"] == 'diff':
                S.op('dve', lambda e: e.memset(Q.ap[32:64, 0, :], 0.0), pwrites=[Q.tok])
                S.op('dve', lambda e: e.memset(Q.ap[0:32, 1, :], 0.0), pwrites=[Q.tok])
            for (src, g, p0, n) in u["qparts"]:
                S.dma('sp', Q.ap[p0:p0 + n, g, :], src, Q.tok, reads=[tok_attn_in], pwrites=[Q.tok])
            S.dma('sp', V.ap[:, :, 0:64], u["V"].rearrange("(kb p) c -> p kb c", p=128), V.tok,
                  reads=[tok_attn_in], pwrites=[V.tok])

        def free_unit(st):
            pass

        def dense_unit(u, st, qchunks):
            K, Q, V = st["K"], st["Q"], st["V"]
            G, kind = u["G"], u["kind"]
            two = (G == 2 or kind == 'diff')
            items = []
            for (q0, nq, kbs) in qchunks:
                pairs = [kbs[i:i + 2] for i in range(0, len(kbs), 2)]
                for pi, pair in enumerate(pairs):
                    items.append((q0, nq, pair, pi == 0, pi == len(pairs) - 1))
            LOOK = 2

            def qk(it):
                q0, nq, pair, first, last = it
                W = nq * (2 if two else 1)
                STt = STb[cnt['st'] % 3]
                cnt['st'] += 1
                for jj, kb in enumerate(pair):
                    base = jj * 512
                    if two:
                        out = STt.ap[:, base:base + W].rearrange("p (g q) -> p g q", g=2)
                        rhs = Q.ap[:, :, q0:q0 + nq]
                    else:
                        out = STt.ap[:, base:base + W]
                        rhs = Q.ap[:, 0, q0:q0 + nq]
                    S.op('pe', lambda e, out=out, kb=kb, rhs=rhs: e.matmul(
                        out, lhsT=K.ap[:, kb * 128:(kb + 1) * 128], rhs=rhs, start=True, stop=True),
                        reads=[K.tok, Q.tok], writes=[STt.tok] if jj == 0 else [], pwrites=[] if jj == 0 else [STt.tok],
                        inc=(jj == len(pair) - 1))
                return STt

            def ex(STt, it):
                q0, nq, pair, first, last = it
                W = nq * (2 if two else 1)
                PTt = PT[cnt['pt'] % 3]
                cnt['pt'] += 1
                npair = len(pair)
                if W == 512:
                    S.op('act', lambda e: e.activation(out=PTt.ap[:, :512 * npair], in_=STt.ap[:, :512 * npair], func=AF.Exp,
                                                       scale=u["scale"]), reads=[STt.tok], writes=[PTt.tok])
                else:
                    iv = STt.ap.rearrange("p (j c) -> p j c", c=512)[:, :npair, :W]
                    ov = PTt.ap.rearrange("p (j c) -> p j c", c=512)[:, :npair, :W]
                    S.op('act', lambda e: e.activation(out=ov, in_=iv, func=AF.Exp, scale=u["scale"]),
                         reads=[STt.tok], writes=[PTt.tok])
                return PTt

            def pv(PTt, it, Oc):
                q0, nq, pair, first, last = it
                W = nq * (2 if two else 1)
                for jj, kb in enumerate(pair):
                    st_ = first and jj == 0
                    sp_ = last and jj == len(pair) - 1
                    S.op('pe', lambda e, kb=kb, jj=jj, st_=st_, sp_=sp_: e.matmul(
                        Oc.ap[:, :W], lhsT=V.ap[:, kb, :], rhs=PTt.ap[:, jj * 512:jj * 512 + W], start=st_, stop=sp_),
                        reads=[V.tok, PTt.tok], writes=[Oc.tok] if st_ else [], pwrites=[] if st_ else [Oc.tok], inc=True)

            def normalise(it, Oc):
                q0, nq, pair, first, last = it
                W = nq * (2 if two else 1)
                r = rz[cnt['rz'] % 2]
                o = ost[cnt['rz'] % 2]
                cnt['rz'] += 1
                S.op('dve', lambda e: e.reciprocal(out=r.ap[64:128, :W], in_=Oc.ap[64:128, :W]), reads=[Oc.tok], writes=[r.tok])
                if kind == 'dense':
                    S.op('dve', lambda e: e.tensor_tensor(out=o.ap[0:64, :W], in0=Oc.ap[0:64, :W], in1=r.ap[64:128, :W], op=ALU.mult),
                         reads=[Oc.tok, r.tok], writes=[o.tok])
                    for g in range(G):
                        S.dma('sp', OT[u["orow"][g]:u["orow"][g] + 64, q0:q0 + nq], o.ap[0:64, g * nq:(g + 1) * nq], o.tok,
                              reads=[o.tok], pwrites=[tok_OT])
                else:
                    S.op('dve', lambda e: e.tensor_tensor(out=o1.ap[0:64, :nq], in0=Oc.ap[0:64, 0:nq], in1=r.ap[64:128, 0:nq], op=ALU.mult),
                         reads=[Oc.tok, r.tok], writes=[o1.tok])
                    S.op('dve', lambda e: e.tensor_tensor(out=o2.ap[0:64, :nq], in0=Oc.ap[0:64, nq:2 * nq], in1=r.ap[64:128, nq:2 * nq],
                                                          op=ALU.mult), reads=[Oc.tok, r.tok], writes=[o2.tok])
                    S.op('dve', lambda e: e.scalar_tensor_tensor(out=o1.ap[0:64, :nq], in0=o2.ap[0:64, :nq], scalar=lamc.ap[0:64, 0:1],
                                                                 in1=o1.ap[0:64, :nq], op0=ALU.mult, op1=ALU.add),
                         reads=[o1.tok, o2.tok, lamc.tok], writes=[o1.tok])
                    S.op('dve', lambda e: e.tensor_tensor(out=osq.ap[0:64, :nq], in0=o1.ap[0:64, :nq], in1=o1.ap[0:64, :nq], op=ALU.mult),
                         reads=[o1.tok], writes=[osq.tok])
                    return lambda: norm2(it, Oc, o)
                return None

            def norm2(it, Oc, o):
                    q0, nq, pair, first, last = it
                    mm_group(Oc.ap[0:64, :nq], [(ones_bf.ap[0:64, 0:64], osq.ap[0:64, :nq])], [ones_bf.tok, osq.tok], Oc.tok)
                    return lambda: norm3(it, Oc, o)

            def norm3(it, Oc, o):
                    q0, nq, pair, first, last = it
                    S.op('act', lambda e: e.activation(out=rsd.ap[0:64, :nq], in_=Oc.ap[0:64, :nq], func=AF.Ln,
                                                       bias=epsc.ap[0:64, :], scale=1.0 / 64),
                         reads=[Oc.tok, epsc.tok], writes=[rsd.tok])
                    S.op('act', lambda e: e.activation(out=rsd.ap[0:64, :nq], in_=rsd.ap[0:64, :nq], func=AF.Exp, scale=-0.5),
                         reads=[rsd.tok], writes=[rsd.tok])
                    S.op('dve', lambda e: e.scalar_tensor_tensor(out=o.ap[0:64, :nq], in0=o1.ap[0:64, :nq], scalar=subg.ap[0:64, 0:1],
                                                                 in1=rsd.ap[0:64, :nq], op0=ALU.mult, op1=ALU.mult),
                         reads=[o1.tok, subg.tok, rsd.tok], writes=[o.tok])
                    S.dma('sp', OT[u["orow"][0]:u["orow"][0] + 64, q0:q0 + nq], o.ap[0:64, :nq], o.tok,
                          reads=[o.tok], pwrites=[tok_OT])

            stq = [qk(items[j]) for j in range(min(LOOK, len(items)))]
            Oc = None
            pending = []
            for j, it in enumerate(items):
                if j + LOOK < len(items):
                    stq.append(qk(items[j + LOOK]))
                PTt = ex(stq[j], it)
                if it[3]:
                    Oc = Ob[cnt['o'] % 2]
                    cnt['o'] += 1
                pv(PTt, it, Oc)
                for pj, pf in list(pending):
                    if pj <= j:
                        nxt = pf()
                        pending.remove((pj, pf))
                        if nxt is not None:
                            pending.append((j + 1, nxt))
                if it[4]:
                    while pending:
                        pj, pf = pending.pop(0)
                        nxt = pf()
                        if nxt is not None:
                            pending.append((j, nxt))
                    f2 = normalise(it, Oc)
                    if f2 is not None:
                        pending.append((j + 6, f2))
            while pending:
                pj, pf = pending.pop(0)
                nxt = pf()
                if nxt is not None:
                    pending.append((pj, nxt))

        def win_unit(u, st, blocks):
            K, Q, V = st["K"], st["Q"], st["V"]
            items = []
            for b in blocks:
                if b < 64:
                    kbs = [(kb, mk) for kb, mk in ((b - 1, 0), (b, None), (b + 1, 1)) if 0 <= kb < 64] + [(64, None), (65, None)]
                else:
                    kbs = [(64, None), (65, None)]
                for g in range(2):
                    items.append((b, g, kbs))
            LOOK = 2

            def qk(it):
                b, g, kbs = it
                q0 = b * 128
                STt = STb[cnt['st'] % 3]
                cnt['st'] += 1
                for i, (kb, mk) in enumerate(kbs):
                    S.op('pe', lambda e, i=i, kb=kb, mk=mk: e.matmul(
                        STt.ap[:, i * 128:(i + 1) * 128], lhsT=K.ap[:, kb * 128:(kb + 1) * 128], rhs=Q.ap[:, g, q0:q0 + 128],
                        start=True, stop=(mk is None)),
                        reads=[K.tok, Q.tok], writes=[STt.tok] if i == 0 else [], pwrites=[] if i == 0 else [STt.tok],
                        inc=(i == len(kbs) - 1 and mk is None))
                    if mk is not None:
                        S.op('pe', lambda e, i=i, mk=mk: e.matmul(
                            STt.ap[:, i * 128:(i + 1) * 128], lhsT=masks.ap[:, 2, :], rhs=masks.ap[:, mk, :],
                            start=False, stop=True),
                            reads=[masks.tok], pwrites=[STt.tok], inc=(i == len(kbs) - 1))
                return STt

            def rest(STt, it):
                b, g, kbs = it
                q0 = b * 128
                n = len(kbs)
                PTt = PT[cnt['pt'] % 3]
                cnt['pt'] += 1
                S.op('act', lambda e: e.activation(out=PTt.ap[:, :128 * n], in_=STt.ap[:, :128 * n], func=AF.Exp, scale=u["scale"]),
                     reads=[STt.tok], writes=[PTt.tok])
                Oc = Ob[cnt['o'] % 2]
                cnt['o'] += 1
                for i, (kb, mk) in enumerate(kbs):
                    S.op('pe', lambda e, kb=kb, i=i: e.matmul(
                        Oc.ap[:, :128], lhsT=V.ap[:, kb, :], rhs=PTt.ap[:, i * 128:(i + 1) * 128], start=(i == 0), stop=False),
                        reads=[V.tok, PTt.tok], writes=[Oc.tok] if i == 0 else [], pwrites=[] if i == 0 else [Oc.tok],
                        inc=False)
                hd = u["heads"][g]
                S.op('pe', lambda e: e.matmul(Oc.ap[:, :128], lhsT=sinkrow.ap[0:1, hd, :], rhs=ones_bf.ap[0:1, 0:128],
                                              start=False, stop=True),
                     reads=[sinkrow.tok, ones_bf.tok], pwrites=[Oc.tok], inc=True)
                r = rz[cnt['rz'] % 2]
                o = ost[cnt['rz'] % 2]
                cnt['rz'] += 1
                S.op('dve', lambda e: e.reciprocal(out=r.ap[64:128, :128], in_=Oc.ap[64:128, :128]), reads=[Oc.tok], writes=[r.tok])
                S.op('dve', lambda e: e.tensor_tensor(out=o.ap[0:64, :128], in0=Oc.ap[0:64, :128], in1=r.ap[64:128, :128], op=ALU.mult),
                     reads=[Oc.tok, r.tok], writes=[o.tok])
                S.dma('sp', OT[u["orow"][g]:u["orow"][g] + 64, q0:q0 + 128], o.ap[0:64, :128], o.tok,
                      reads=[o.tok], pwrites=[tok_OT])

            stq = [qk(items[j]) for j in range(min(LOOK, len(items)))]
            for j, it in enumerate(items):
                if j + LOOK < len(items):
                    stq.append(qk(items[j + LOOK]))
                rest(stq[j], it)

        if dbg and "attn_units" in dbg:
            units = [units[i_] for i_ in dbg["attn_units"]]
        nqlim = dbg.get("attn_nq", 10 ** 9) if dbg else 10 ** 9
        load_unit(units[0], sets[0])
        for ui, u in enumerate(units):
            st = sets[ui % 2]
            if ui + 1 < len(units):
                load_unit(units[ui + 1], sets[(ui + 1) % 2])
            allk = list(range(NKB))
            if u["kind"] == 'win':
                win_unit(u, st, [b for b in range(64 + (2 if need_ctx else 0)) if (b < nqlim or b >= 64)])
            else:
                nq = 256 if (u["G"] == 2 or u["kind"] == 'diff') else 512
                qch = [(q0, nq, allk) for q0 in range(0, TL, nq) if q0 // nq < nqlim]
                if need_ctx and nqlim > 0:
                    qch.append((TL, 256, [64, 65]))
                if qch:
                    dense_unit(u, st, qch)

    def phase_merge(l, chunks):
        P.reset()
        S.barrier()
        Wgt = B(P.alloc([128, 8, 4096], BF16), "Wgt")
        Wbr = B(P.alloc([128, 8, D], BF16), "Wbr")
        Wo = B(P.alloc([128, 8, D], BF16), "Wo")
        load_w_cast(Wgt, wgate_d[l].rearrange("(c p) j -> p c j", p=128))
        load_w_cast(Wbr, wbr_d[l].rearrange("i (c p) j -> p (i c) j", p=128))
        load_w_cast(Wo, wout_d[l].rearrange("(c p) j -> p c j", p=128))
        xb = [B(P.alloc([128, 8, 512], F32), f"xb{k}") for k in range(2)]
        otb = [B(P.alloc([128, 8, 512], BF16), f"ot{k}") for k in range(2)]
        ubs = [B(P.alloc([128, 8, 512], BF16), f"ub{k}") for k in range(2)]
        rs2 = B(P.alloc([128, 512], F32), "rs2")
        yacc = B(P.alloc([128, 8, 512], F32), "yacc")
        ybf = B(P.alloc([128, 8, 512], BF16), "ybf")
        rs = B(P.alloc([128, 512], F32), "rs")
        tmps = [B(P.alloc([128, 512], F32), f"tmp{k}") for k in range(2)]
        sig = [B(P.alloc([128, 512], F32), f"sig{k}") for k in range(2)]
        ssb = bank[0]
        ssb2 = bank[7]
        glb = [bank[1], bank[2]]
        zb = [bank[3], bank[4]]
        y2b = [bank[5], bank[6]]
        oview = OT.rearrange("(c p) t -> p c t", p=128)
        it = [0]

        def loads(n):
            ci = chunks[n]
            t0, N, w = chunk(ci)
            load_x(xb[n % 2], ci)
            S.dma('sp', otb[n % 2].ap[:, :, :N], oview[:, :, t0:t0 + N], otb[n % 2].tok, reads=[tok_OT], writes=[otb[n % 2].tok])
        loads(0)
        for n, ci in enumerate(chunks):
            t0, N, w = chunk(ci)
            k = n % 2
            if n + 1 < len(chunks):
                loads(n + 1)
            ub = ubs[k]
            if n == 0:
                adaln(xb[k], ub, N, w, 1, ssb2, rs, tmps)
            for d in range(8):
                if n + 1 < len(chunks) and d >= 3:
                    t0n, Nn, wn = chunk(chunks[n + 1])
                    k1 = (n + 1) % 2
                    if d == 3:
                        adaln_sq(xb[k1], ubs[k1], Nn)
                    elif d == 4:
                        adaln_ss(ubs[k1], Nn, ssb2)
                    elif d == 5:
                        adaln_fin_start(Nn, ssb2, rs)
                    else:
                        for c_ in range((d - 6) * 4, (d - 6) * 4 + 4):
                            adaln_fin_c(xb[k1], ubs[k1], Nn, wn, 1, rs, tmps, c_)
                for i in range(4):
                    gl = glb[it[0] % 2]
                    z = zb[it[0] % 2]
                    sg = sig[it[0] % 2]
                    it[0] += 1
                    mm_group(gl.ap[:, :N], [(Wgt.ap[:, kc, i * 1024 + d * 128:i * 1024 + (d + 1) * 128], ub.ap[:, kc, :N])
                                            for kc in range(8)], [Wgt.tok, ub.tok], gl.tok)
                    mm_group(z.ap[:, :N], [(Wbr.ap[:, i * 2 + kc, d * 128:(d + 1) * 128], otb[k].ap[:, i * 2 + kc, :N])
                                           for kc in range(2)], [Wbr.tok, otb[k].tok], z.tok)
                    S.op('act', lambda e, gl=gl, sg=sg: e.activation(out=sg.ap[:, :N], in_=gl.ap[:, :N], func=AF.Sigmoid),
                         reads=[gl.tok], writes=[sg.tok])
                    if i == 0:
                        S.op('dve', lambda e, d=d, z=z, sg=sg: e.tensor_tensor(out=yacc.ap[:, d, :N], in0=z.ap[:, :N], in1=sg.ap[:, :N],
                                                                               op=ALU.mult),
                             reads=[z.tok, sg.tok], writes=[yacc.tok] if d == 0 else [], pwrites=[] if d == 0 else [yacc.tok])
                    else:
                        S.op('dve', lambda e, z=z, sg=sg: e.tensor_tensor(out=sg.ap[:, :N], in0=z.ap[:, :N], in1=sg.ap[:, :N],
                                                                         op=ALU.mult), reads=[z.tok, sg.tok], writes=[sg.tok])
                        S.op(EW2, lambda e, d=d, sg=sg: e.tensor_tensor(out=yacc.ap[:, d, :N], in0=yacc.ap[:, d, :N],
                                                                          in1=sg.ap[:, :N], op=ALU.add),
                             reads=[sg.tok, yacc.tok], writes=[yacc.tok])
                S.op('act', lambda e, d=d: e.activation(out=ybf.ap[:, d, :N], in_=yacc.ap[:, d, :N], func=AF.Copy),
                     reads=[yacc.tok], writes=[ybf.tok] if d == 0 else [], pwrites=[] if d == 0 else [ybf.tok])
            for d2 in range(8):
                y2 = y2b[d2 % 2]
                mm_group(y2.ap[:, :N], [(Wo.ap[:, dd, d2 * 128:(d2 + 1) * 128], ybf.ap[:, dd, :N]) for dd in range(8)],
                         [Wo.tok, ybf.tok], y2.tok)
                S.op('act', lambda e, d2=d2, y2=y2: e.activation(out=ub.ap[:, d2, :N], in_=y2.ap[:, :N], func=AF.Square),
                     reads=[y2.tok], writes=[ub.tok] if d2 == 0 else [], pwrites=[] if d2 == 0 else [ub.tok])
                S.op('dve', lambda e, d2=d2, y2=y2: e.tensor_copy(out=yacc.ap[:, d2, :N], in_=y2.ap[:, :N]),
                     reads=[y2.tok, ub.tok], writes=[yacc.tok] if d2 == 0 else [], pwrites=[] if d2 == 0 else [yacc.tok])
            mm_group(ssb.ap[:, :N], [(ones_bf.ap, ub.ap[:, c, :N]) for c in range(8)], [ones_bf.tok, ub.tok], ssb.tok)
            rstd_from_ss(ssb, N, rs2, 1.0 / D)
            for d in range(8):
                tmp = tmps[d % 2]
                S.op('dve', lambda e, d=d, tmp=tmp: e.scalar_tensor_tensor(
                    out=tmp.ap[:, :N], in0=yacc.ap[:, d, :N], scalar=Gvec.ap[:, w, 8 + d:8 + d + 1],
                    in1=rs2.ap[:, :N], op0=ALU.mult, op1=ALU.mult),
                    reads=[yacc.tok, Gvec.tok, rs2.tok], writes=[tmp.tok])
                S.op(EW2, lambda e, d=d, tmp=tmp: e.tensor_tensor(out=xb[k].ap[:, d, :N], in0=xb[k].ap[:, d, :N],
                                                                    in1=tmp.ap[:, :N], op=ALU.add),
                     reads=[tmp.tok, xb[k].tok], writes=[xb[k].tok])
            S.dma('sp', xview[:, :, t0:t0 + N], xb[k].ap[:, :, :N], xb[k].tok, reads=[xb[k].tok], writes=[xtok[ci]])

    stop_after = dbg.get("stop_after") if dbg else None
    xin_view = xT_in.rearrange("(c p) t -> p c t", p=128)
    allc = list(range(NCH))
    latc = list(range(16))
    if dbg and "chunks" in dbg:
        allc = list(dbg["chunks"])
        latc = [c_ for c_ in allc if c_ < 16]
    done = False
    def plan():
        for l in range(DEPTH):
            need_ctx = l < DEPTH - 1
            yield ("mod", l), (lambda l=l: phase_mod(l))
            yield ("ffn1a", l), (lambda l=l: phase_ffn_a(l, 0, 0, allc, src_first=(xin_view if l == 0 else None)))
            yield ("ffn1", l), (lambda l=l: phase_ffn_b(l, 0, 0, allc, src_first=(xin_view if l == 0 else None)))
            yield ("proj", l), (lambda l=l: phase_proj(l, allc))
            yield ("attn", l), (lambda l=l, need_ctx=need_ctx: phase_attn(l, need_ctx))
            ch2 = allc if need_ctx else latc
            yield ("merge", l), (lambda l=l, ch2=ch2: phase_merge(l, ch2))
            yield ("ffn2a", l), (lambda l=l, ch2=ch2: phase_ffn_a(l, 1, 2, ch2))
            yield ("ffn2", l), (lambda l=l, ch2=ch2: phase_ffn_b(l, 1, 2, ch2, final_out=(l == DEPTH - 1)))
    for name, fn in plan():
        fn()
        if stop_after == name:
            break
    S.barrier()
    S.emit_all()
    return nc, S


def _perm(n_block, half):
    idx = np.arange(n_block)
    r = idx % half
    base = idx - r
    return base + (r + half // 2) % half


def _rope_tables(rot_dim):
    half = rot_dim // 2
    inv = 10000.0 ** (-np.arange(0, half, 2, dtype=np.float64) / half)
    t = np.arange(TL)
    row = (t // 64).astype(np.float64)
    col = (t % 64).astype(np.float64)
    nfr = half // 2
    C = np.ones((rot_dim, T), np.float64)
    Sg = np.zeros((rot_dim, T), np.float64)
    for i in range(rot_dim):
        r = i % half
        pos = row if i < half else col
        ang = pos * inv[r % nfr]
        C[i, :TL] = np.cos(ang)
        Sg[i, :TL] = np.sin(ang) * (-1.0 if r < nfr else 1.0)
    rep = 128 // rot_dim
    return np.tile(C, (rep, 1)).astype(np.float32), np.tile(Sg, (rep, 1)).astype(np.float32)


def _prep_common(inp):
    f = np.float32
    w_in = np.asarray(inp["w_in"], f)
    p64 = _perm(64, 32)
    p32 = _perm(32, 16)
    fm_cols = np.concatenate([np.arange(0, 384), np.arange(928, 1440), np.arange(1696, 2080),
                              np.arange(512, 896)])
    perm_fm = np.concatenate([np.concatenate([b0 + p64 for b0 in range(0, 384, 64)]),
                              np.concatenate([928 + b0 + p32 for b0 in range(0, 512, 32)]),
                              np.concatenate([1696 + b0 + p64 for b0 in range(0, 384, 64)])])
    kpe_cols = np.concatenate([np.arange(896, 928), 896 + p32])
    v_cols = np.concatenate([np.arange(384, 512), np.arange(1440, 1696), np.arange(2080, 2208)])
    uq = np.asarray(inp["mla_w_uq"], f)
    ukv = np.asarray(inp["mla_w_ukv"], f)
    n_cols = np.concatenate([np.arange(h * 96, h * 96 + 64) for h in range(4)])
    p_cols = np.concatenate([np.arange(h * 96 + 64, h * 96 + 96) for h in range(4)])
    pr_cols = np.concatenate([h * 96 + 64 + p32 for h in range(4)])
    kk_cols = np.concatenate([np.arange(h * 128, h * 128 + 64) for h in range(4)])
    kv_cols = np.concatenate([np.arange(h * 128 + 64, h * 128 + 128) for h in range(4)])
    b_mod = np.asarray(inp["b_mod"], f)
    g_pre = np.asarray(inp["g_pre"], f)
    g_post = np.asarray(inp["g_post"], f)
    gq = np.asarray(inp["gqa_q_norm"], f)
    gk = np.asarray(inp["gqa_k_norm"], f)
    mq = np.asarray(inp["mla_q_norm"], f)
    mkv = np.asarray(inp["mla_kv_norm"], f)
    sub = np.asarray(inp["diff_subln"], f)
    vecT = np.zeros((DEPTH, 128, NV), f)
    for l in range(DEPTH):
        vecT[l, :, 0:24] = g_pre[l].reshape(24, 128).T
        vecT[l, :, 24:48] = g_post[l].reshape(24, 128).T
        vecT[l, :, 48] = np.tile(gq[l], 2)
        vecT[l, :, 49] = np.tile(gq[l][p64], 2)
        vecT[l, :, 50] = np.tile(gk[l], 2)
        vecT[l, :, 51] = np.tile(gk[l][p64], 2)
        vecT[l, :, 52:54] = mq[l].reshape(2, 128).T
        vecT[l, :, 54] = mkv[l]
        vecT[l, :, 55] = np.tile(sub[l], 2)
    C64, S64 = _rope_tables(64)
    C32, S32 = _rope_tables(32)
    kj = np.arange(128)[:, None]
    qi = np.arange(128)[None, :]
    m_lo = np.where(kj >= qi, 0.0, -30000.0).astype(np.float32)
    m_hi = np.where(kj <= qi, 0.0, -30000.0).astype(np.float32)
    masks = np.stack([m_lo, m_hi, np.eye(128, dtype=np.float32)], axis=1).astype(ml_dtypes.bfloat16)
    c = np.ascontiguousarray
    return {
        "w_mod": c(np.asarray(inp["w_mod"], f)),
        "bmodT": c(b_mod.reshape(DEPTH, 72, 128).transpose(0, 2, 1)),
        "vecT": vecT,
        "sinkB": c(np.broadcast_to(np.asarray(inp["swa_sink"], f)[:, None, :], (DEPTH, 128, 4))),
        "lamB": c(np.broadcast_to(np.asarray(inp["diff_lambda"], f).reshape(DEPTH, 1, 128), (DEPTH, 128, 128))),
        "w_ffn_gate": c(np.asarray(inp["w_ffn_gate"], f)),
        "w_ffn_up": c(np.asarray(inp["w_ffn_up"], f)),
        "w_ffn_down": c(np.asarray(inp["w_ffn_down"], f)),
        "w_fm": c(w_in[:, :, fm_cols]),
        "w_fmr": c(w_in[:, :, perm_fm]),
        "w_kpe": c(w_in[:, :, kpe_cols]),
        "w_v": c(w_in[:, :, v_cols]),
        "w_gate": c(w_in[:, :, 2208:]),
        "wuq_n": c(uq[:, :, n_cols]),
        "wuq_p": c(uq[:, :, p_cols]),
        "wuq_pr": c(uq[:, :, pr_cols]),
        "wukv_k": c(ukv[:, :, kk_cols]),
        "wukv_v": c(ukv[:, :, kv_cols]),
        "w_branch": c(np.asarray(inp["w_branch"], f)),
        "w_out": c(np.asarray(inp["w_out"], f)),
        "ropeC64": C64, "ropeS64": S64, "ropeC32": C32, "ropeS32": S32,
        "masks": masks,
    }


def _prep_core(inp, b):
    f = np.float32
    xT = np.concatenate([np.asarray(inp["x"][b], f).T, np.asarray(inp["ctx"][b], f).T], axis=1)
    ccT = np.stack([np.asarray(inp["c"][b], f).reshape(8, 128).T, np.asarray(inp["c_ctx"], f).reshape(8, 128).T], axis=1)
    return {"xT": np.ascontiguousarray(xT), "ccT": np.ascontiguousarray(ccT)}


_CACHE = {}


def kernel(**inputs):
    if "nc" not in _CACHE:
        _CACHE["nc"] = build_program()[0]
    nc = _CACHE["nc"]
    common = _prep_common(inputs)
    in_maps = []
    for b in range(8):
        m = dict(common)
        m.update(_prep_core(inputs, b))
        in_maps.append(m)
    res = run_bass_kernel_spmd(nc, in_maps, core_ids=list(range(8)))
    out = np.stack([np.ascontiguousarray(np.asarray(r["outT"], np.float32).T) for r in res.results], axis=0)
    return out
```
